# Optimizing a Trainium2 kernel written in Bass

```python
import math
import jax, jax.numpy as jnp
from jax import lax
import numpy as np

D_MODEL = 2048
BATCH = 16
SEQ = 2048
DEPTH = 4

HEAD_DIM = 128
A_HEADS = 4
A_QK_DIM = 64
A_V_DIM = 2 * A_QK_DIM
B_HEADS = 4
C_HEADS = 6
C_PATTERNS = ((128, 1), (512, 4), (2048, 16))
BLOCK = 128
D_FF = 4 * D_MODEL
N_BRANCH = 3
EPS = 1e-6
NEG_INF = -1e30

A_QK_W = A_HEADS * 2 * A_QK_DIM
A_WIDTH = A_HEADS * A_V_DIM
B_WIDTH = B_HEADS * HEAD_DIM
C_WIDTH = C_HEADS * HEAD_DIM
PROJ_SPLITS = (A_QK_W, A_QK_W, A_WIDTH,
               B_WIDTH, B_WIDTH, B_WIDTH, B_HEADS,
               C_WIDTH, C_WIDTH, C_WIDTH,
               N_BRANCH * D_MODEL)
IN_COLS = sum(PROJ_SPLITS)

kernel_name = "hybrid_diff_fox_dilated_gated_block"


def _rms(x, g):
    x32 = x.astype(jnp.float32)
    y = x32 * lax.rsqrt(jnp.mean(x32 * x32, axis=-1, keepdims=True) + EPS)
    return (y * g.astype(jnp.float32)).astype(x.dtype)


def _alibi_slopes(n):
    return 2.0 ** (-8.0 * jnp.arange(1, n + 1, dtype=jnp.float32) / n)


def _diff_attention(q, k, v, lam, slopes):
    S = q.shape[1]
    scale = A_QK_DIM ** -0.5
    q = q.astype(jnp.float32); k = k.astype(jnp.float32); v = v.astype(jnp.float32)
    outs = []
    for blk in range(S // BLOCK):
        q0 = blk * BLOCK
        nk = q0 + BLOCK
        dist = (q0 + jnp.arange(BLOCK))[:, None] - jnp.arange(nk)[None, :]
        bias = -slopes[:, None, None] * dist.astype(jnp.float32)
        s = jnp.einsum('bqhmd,bkhmd->bhmqk', q[:, q0:nk], k[:, :nk]) * scale + bias[None, :, None]
        s = jnp.where(dist >= 0, s, NEG_INF)
        p = jax.nn.softmax(s, axis=-1)
        p = p[:, :, 0] - lam * p[:, :, 1]
        outs.append(jnp.einsum('bhqk,bkhe->bqhe', p, v[:, :nk]))
    return jnp.concatenate(outs, axis=1)


def _forgetting_attention(q, k, v, logf_cum):
    S = q.shape[1]
    scale = HEAD_DIM ** -0.5
    q = q.astype(jnp.float32); k = k.astype(jnp.float32); v = v.astype(jnp.float32)
    c = jnp.transpose(logf_cum, (0, 2, 1))
    outs = []
    for blk in range(S // BLOCK):
        q0 = blk * BLOCK
        nk = q0 + BLOCK
        dist = (q0 + jnp.arange(BLOCK))[:, None] - jnp.arange(nk)[None, :]
        decay = c[:, :, q0:nk, None] - c[:, :, None, :nk]
        s = jnp.einsum('bqhd,bkhd->bhqk', q[:, q0:nk], k[:, :nk]) * scale + decay
        s = jnp.where(dist >= 0, s, NEG_INF)
        p = jax.nn.softmax(s, axis=-1)
        outs.append(jnp.einsum('bhqk,bkhd->bqhd', p, v[:, :nk]))
    return jnp.concatenate(outs, axis=1)


def _dilated_pattern(q, k, v, slopes, window, dilation):
    B, S, H, dh = q.shape
    L = S // dilation
    nw = window // dilation
    nb = -(-L // BLOCK)
    Lp = nb * BLOCK
    scale = dh ** -0.5

    def to_sub(t):
        t = t.astype(jnp.float32).reshape(B, L, dilation, H, dh).transpose(0, 2, 3, 1, 4)
        t = jnp.pad(t, ((0, 0), (0, 0), (0, 0), (0, Lp - L), (0, 0)))
        return t.reshape(B, dilation, H, nb, BLOCK, dh)

    def band(t):
        t = jnp.pad(t, ((0, 0), (0, 0), (0, 0), (1, 0), (0, 0), (0, 0)))
        return jnp.concatenate([t[:, :, :, :-1], t[:, :, :, 1:]], axis=4)

    qs = to_sub(q)
    kb = band(to_sub(k))
    vb = band(to_sub(v))
    i = jnp.arange(BLOCK)[:, None]
    j = jnp.arange(2 * BLOCK)[None, :]
    dist = BLOCK + i - j
    blk = jnp.arange(nb)[:, None, None]
    valid = (dist >= 0) & (dist <= nw) & ((blk > 0) | (j >= BLOCK))
    bias = -slopes[:, None, None, None] * (dist * dilation).astype(jnp.float32)
    s = jnp.einsum('brhnqd,brhnkd->brhnqk', qs, kb) * scale + bias[None, None]
    s = jnp.where(valid, s, NEG_INF)
    m = jnp.max(s, axis=-1)
    p = jnp.exp(s - m[..., None])
    l = jnp.sum(p, axis=-1)
    o = jnp.einsum('brhnqk,brhnkd->brhnqd', p, vb) / l[..., None]

    def from_sub(t):
        rest = t.shape[5:]
        t = t.reshape((B, dilation, H, Lp) + rest)[:, :, :, :L]
        t = jnp.moveaxis(t, 3, 1)
        return t.reshape((B, S, H) + rest)

    return from_sub(o), from_sub(m), from_sub(l)


def _dilated_mixture(q, k, v, slopes):
    parts = [_dilated_pattern(q, k, v, slopes, w, d) for (w, d) in C_PATTERNS]
    m_all = jnp.stack([p[1] for p in parts])
    m_top = jnp.max(m_all, axis=0)
    wts = jnp.stack([p[2] * jnp.exp(p[1] - m_top) for p in parts])
    o_all = jnp.stack([p[0] for p in parts])
    return jnp.sum(wts[..., None] * o_all, axis=0) / jnp.sum(wts, axis=0)[..., None]


def setup_inputs(seed: int = 0) -> dict:
    key = jax.random.key(seed)
    ks = jax.random.split(key, 24)

    def nrm(k, shape, scale):
        return jax.random.normal(k, shape, jnp.float32) * scale

    def gain(k, shape):
        return 1.0 + 0.02 * jax.random.normal(k, shape, jnp.float32)

    return {
        "x": nrm(ks[0], (BATCH, SEQ, D_MODEL), 1.0),
        "g_mix_norm": gain(ks[1], (DEPTH, D_MODEL)),
        "w_in": nrm(ks[2], (DEPTH, D_MODEL, IN_COLS), D_MODEL ** -0.5),
        "b_forget": 2.0 + 0.1 * jax.random.normal(ks[3], (DEPTH, B_HEADS), jnp.float32),
        "b_gate": nrm(ks[4], (DEPTH, N_BRANCH * D_MODEL), 0.02),
        "g_q_a": gain(ks[5], (DEPTH, A_QK_DIM)),
        "g_k_a": gain(ks[6], (DEPTH, A_QK_DIM)),
        "lam_q1": nrm(ks[7], (DEPTH, A_QK_DIM), 0.1),
        "lam_k1": nrm(ks[8], (DEPTH, A_QK_DIM), 0.1),
        "lam_q2": nrm(ks[9], (DEPTH, A_QK_DIM), 0.1),
        "lam_k2": nrm(ks[10], (DEPTH, A_QK_DIM), 0.1),
        "g_sub_a": gain(ks[11], (DEPTH, A_V_DIM)),
        "g_q_b": gain(ks[12], (DEPTH, HEAD_DIM)),
        "g_k_b": gain(ks[13], (DEPTH, HEAD_DIM)),
        "g_q_c": gain(ks[14], (DEPTH, HEAD_DIM)),
        "g_k_c": gain(ks[15], (DEPTH, HEAD_DIM)),
        "w_up_a": nrm(ks[16], (DEPTH, A_WIDTH, D_MODEL), A_WIDTH ** -0.5),
        "w_up_b": nrm(ks[17], (DEPTH, B_WIDTH, D_MODEL), B_WIDTH ** -0.5),
        "w_up_c": nrm(ks[18], (DEPTH, C_WIDTH, D_MODEL), C_WIDTH ** -0.5),
        "w_out": nrm(ks[19], (DEPTH, D_MODEL, D_MODEL), D_MODEL ** -0.5),
        "g_ffn_norm": gain(ks[20], (DEPTH, D_MODEL)),
        "w_ff_in": nrm(ks[21], (DEPTH, D_MODEL, D_FF), D_MODEL ** -0.5),
        "w_ff_out": nrm(ks[22], (DEPTH, D_FF, D_MODEL), D_FF ** -0.5),
    }


def reference(x, g_mix_norm, w_in, b_forget, b_gate, g_q_a, g_k_a, lam_q1, lam_k1, lam_q2, lam_k2,
              g_sub_a, g_q_b, g_k_b, g_q_c, g_k_c, w_up_a, w_up_b, w_up_c, w_out,
              g_ffn_norm, w_ff_in, w_ff_out):
    B, S, _ = x.shape
    dt = x.dtype
    split_idx = [int(v) for v in np.cumsum(PROJ_SPLITS)[:-1]]
    slopes_a = _alibi_slopes(A_HEADS)
    slopes_c = _alibi_slopes(C_HEADS)
    for layer in range(DEPTH):
        lam_init = 0.8 - 0.6 * math.exp(-0.3 * layer)
        h = _rms(x, g_mix_norm[layer])
        z = h @ w_in[layer]
        qa, ka, va, qb, kb, vb, fb, qc, kc, vc, gl = jnp.split(z, split_idx, axis=-1)

        qa = _rms(qa.reshape(B, S, A_HEADS, 2, A_QK_DIM), g_q_a[layer])
        ka = _rms(ka.reshape(B, S, A_HEADS, 2, A_QK_DIM), g_k_a[layer])
        va = va.reshape(B, S, A_HEADS, A_V_DIM)
        lam = (jnp.exp(jnp.sum(lam_q1[layer].astype(jnp.float32) * lam_k1[layer].astype(jnp.float32)))
               - jnp.exp(jnp.sum(lam_q2[layer].astype(jnp.float32) * lam_k2[layer].astype(jnp.float32)))
               + lam_init)
        oa = _diff_attention(qa, ka, va, lam, slopes_a)
        oa = (_rms(oa, g_sub_a[layer]) * (1.0 - lam_init)).reshape(B, S, A_WIDTH).astype(dt)

        qb = _rms(qb.reshape(B, S, B_HEADS, HEAD_DIM), g_q_b[layer])
        kb = _rms(kb.reshape(B, S, B_HEADS, HEAD_DIM), g_k_b[layer])
        vb = vb.reshape(B, S, B_HEADS, HEAD_DIM)
        logf = jax.nn.log_sigmoid(fb.astype(jnp.float32) + b_forget[layer].astype(jnp.float32))
        logf_cum = jnp.cumsum(logf, axis=1)
        ob = _forgetting_attention(qb, kb, vb, logf_cum).reshape(B, S, B_WIDTH).astype(dt)

        qc = _rms(qc.reshape(B, S, C_HEADS, HEAD_DIM), g_q_c[layer])
        kc = _rms(kc.reshape(B, S, C_HEADS, HEAD_DIM), g_k_c[layer])
        vc = vc.reshape(B, S, C_HEADS, HEAD_DIM)
        oc = _dilated_mixture(qc, kc, vc, slopes_c).reshape(B, S, C_WIDTH).astype(dt)

        gates = jax.nn.sigmoid(gl + b_gate[layer]).reshape(B, S, N_BRANCH, D_MODEL)
        merged = (gates[:, :, 0] * (oa @ w_up_a[layer])
                  + gates[:, :, 1] * (ob @ w_up_b[layer])
                  + gates[:, :, 2] * (oc @ w_up_c[layer]))
        x = x + merged @ w_out[layer]

        h2 = _rms(x, g_ffn_norm[layer])
        x = x + jnp.square(jax.nn.relu(h2 @ w_ff_in[layer])) @ w_ff_out[layer]
    return x
```

```python
import math
import numpy as np
import ml_dtypes
import concourse.bass as bass
import concourse.mybir as mybir
from concourse.bass_utils import run_bass_kernel_spmd

F32 = mybir.dt.float32
BF16 = mybir.dt.bfloat16
AF = mybir.ActivationFunctionType
ALU = mybir.AluOpType
AXX = mybir.AxisListType.X

D = 2048
S = 2048
DEPTH = 4
IN_COLS = 11524
C_QA, C_KA, C_VA, C_QB, C_KB, C_VB, C_FB, C_QC, C_KC, C_VC, C_GL = (
    0, 512, 1024, 1536, 2048, 2560, 3072, 3076, 3844, 4612, 5380)
EPS = 1e-6
NW = 4
NF = 8
NHB = 4
NPT = 4
VST = 132
VOFF = 8 * 2048

P_GMIX, P_GFFN, P_BGATE, P_GQA, P_GKA, P_GQB, P_GKB, P_GQC, P_GKC = 0, 16, 32, 80, 81, 82, 83, 84, 85
P_LQ1, P_LK1, P_LQ2, P_LK2, P_GSUB, P_BF, NP = 86, 150, 214, 278, 342, 470, 534


class Op:
    __slots__ = ("eng", "fn", "deps", "dma", "val", "need", "idx", "stream", "xw")


class Sched:
    ENGS = ("pe", "act", "dve", "pool", "sp")

    def __init__(self):
        self.ops = {k: [] for k in self.ENGS}
        self.lastw = {}
        self.rdr = {}
        self.streams = []

    def new_stream(self):
        self.streams.append(0)
        return len(self.streams) - 1

    def add(self, eng, fn, r=(), w=(), dma=None):
        op = Op()
        op.eng = eng
        op.fn = fn
        op.dma = dma is not None
        op.need = False
        op.xw = None
        op.idx = None
        op.stream = dma
        op.val = None
        if op.dma:
            self.streams[dma] += 1
            op.val = 16 * self.streams[dma]
        deps = []
        seen = set()

        def push(d, raw):
            if d is None or id(d) in seen:
                return
            if (not d.dma) and d.eng == eng and (not op.dma):
                if eng == "pe" or not raw:
                    return
            seen.add(id(d))
            deps.append(d)

        lastw = self.lastw
        rdr = self.rdr
        for k in r:
            push(lastw.get(k), True)
        for k in w:
            push(lastw.get(k), False)
            rd = rdr.get(k)
            if rd:
                for d in rd.values():
                    push(d, False)
        op.deps = deps
        for k in r:
            rd = rdr.get(k)
            if rd is None:
                rd = rdr[k] = {}
            rd[id(op) if op.dma else eng] = op
        for k in w:
            lastw[k] = op
            rdr[k] = {}
        self.ops[eng].append(op)
        return op

    def finalize(self):
        for ops in self.ops.values():
            for op in ops:
                for d in op.deps:
                    d.need = True
        for ops in self.ops.values():
            cnt = 0
            for op in ops:
                if op.need and not op.dma:
                    cnt += 1
                    op.idx = cnt

    def emit(self, eng, e, eng_sems, stream_sems):
        seen = {}
        for op in self.ops[eng]:
            if op.xw:
                for (st, v) in op.xw:
                    e.wait_ge(stream_sems[st], v)
            for d in op.deps:
                if d.dma:
                    key = ("s", d.stream)
                    val = d.val
                    sem = stream_sems[d.stream]
                else:
                    key = ("e", d.eng)
                    val = d.idx
                    sem = eng_sems[d.eng]
                if seen.get(key, 0) >= val:
                    continue
                seen[key] = val
                e.wait_ge(sem, val)
            if op.fn is None:
                continue
            inst = op.fn(e)
            if op.dma:
                inst.then_inc(stream_sems[op.stream], 16)
            elif op.need:
                inst.then_inc(eng_sems[eng], 1)


def batch_final(sc, ops):
    fin = 16 * sc.streams[ops[0].stream]
    for o in ops:
        o.val = fin


def bkeys(lo, hi):
    return [("B", g) for g in range(lo // 512, (hi - 1) // 512 + 1)]


class Builder:
    def __init__(self, n_layers=DEPTH, n_seq=2, taps=(), stop=99):
        self.stop = stop
        self.tap_group = 0
        self.n_layers = n_layers
        self.n_seq = n_seq
        self.taps = set(taps)
        self.tap_list = []
        self.nc = bass.Bass("TRN2", target_bir_lowering=False)
        self.sc = Sched()

    def dram(self):
        nc = self.nc
        NL, NS = self.n_layers, self.n_seq

        def inp(name, shape, dt=F32):
            return nc.dram_tensor(name, shape, dt, kind="ExternalInput").ap()

        def internal(name, shape, dt=BF16):
            return nc.dram_tensor(name, shape, dt, kind="Internal").ap()

        self.xT = inp("xT", [NS, D, S])
        self.yT = nc.dram_tensor("yT", [NS, D, S], F32, kind="ExternalOutput").ap()
        self.w_in = inp("w_in", [DEPTH, D, IN_COLS])
        self.w_up_a = inp("w_up_a", [DEPTH, 512, D])
        self.w_up_b = inp("w_up_b", [DEPTH, 512, D])
        self.w_up_c = inp("w_up_c", [DEPTH, 768, D])
        self.w_out = inp("w_out", [DEPTH, D, D])
        self.w_ffi = inp("w_ff_in", [DEPTH, D, 4 * D])
        self.w_ffo = inp("w_ff_out", [DEPTH, 4 * D, D])
        self.pvec = inp("pvec", [DEPTH, 128, NP])
        self.cmask = inp("cmask", [128, 17 * 128], BF16)
        self.calibi = inp("calibi", [128, 160])
        self.cmatb = inp("cmatb", [128, 4 * 128], BF16)
        self.cmatf = inp("cmatf", [128, 2 * 128])
        self.wb_in = internal("wb_in", [NL, D, IN_COLS])
        self.wb_up_a = internal("wb_up_a", [NL, 512, D])
        self.wb_up_b = internal("wb_up_b", [NL, 512, D])
        self.wb_up_c = internal("wb_up_c", [NL, 768, D])
        self.wb_out = internal("wb_out", [NL, D, D])
        self.wb_ffi = internal("wb_ffi", [NL, D, 4 * D])
        self.wb_ffo = internal("wb_ffo", [NL, 4 * D, D])
        self.oT_sp = internal("oT_sp", [NS, 14 * 128, S])
        self.mT_sp = internal("mT_sp", [NS, D, S])
        self.tap_out = {}
        for name, shape, dt in self.tap_specs():
            if name in self.taps:
                self.tap_out[name] = nc.dram_tensor("tap_" + name, shape, dt, kind="ExternalOutput").ap()

    def tap_specs(self):
        return [("hT", [128, 16 * 2048], BF16), ("BB", [128, 32768], BF16), ("biasB", [128, 1024], F32),
                ("lg", [128, 6 * 64], F32), ("oT", [14 * 128, 2048], BF16), ("mT", [2048, 2048], BF16)]

    def build(self):
        nc = self.nc
        self.dram()
        from contextlib import ExitStack
        with ExitStack() as es:
            def sb(name, shape, dt):
                return es.enter_context(nc.sbuf_tensor(name, shape, dt))

            def ps(name, shape, dt):
                return es.enter_context(nc.psum_tensor(name, shape, dt))

            import os as _os
            if int(_os.environ.get("KSMALL", "0")):
                BH = sb("BH", [128, 16, 256], BF16)
                BB = sb("BB", [128, 4096], BF16)
            else:
                BH = sb("BH", [128, 16, 2048], BF16)
                BB = sb("BB", [128, 32768], BF16)
            WS = sb("WS", [128, NW, 16, 256], BF16)
            FS = sb("FS", [128, NF, 512], F32)
            HB = sb("HB", [128, NHB, 512], BF16)
            PT = sb("PT", [128, NPT, 128], BF16)
            PTW = sb("PTW", [128, 3, 512], BF16)
            self.PTW = PTW
            OTL = sb("OTL", [128, 4, 128], BF16)
            N1 = sb("N1", [128, 4, 128], F32)
            DT = sb("DT", [128, 4, 128], F32)
            JK = sb("JK", [128, 128], F32)
            JK2 = sb("JK2", [128, 128], F32)
            SCL = sb("SCL", [128, 64], F32)
            MASK = sb("MASK", [128, 17 * 128], BF16)
            ALIBI = sb("ALIBI", [128, 160], F32)
            CMB = sb("CMB", [128, 4 * 128], BF16)
            CMF = sb("CMF", [128, 2 * 128], F32)
            PV = sb("PV", [128, NP], F32)
            GSUBK = sb("GSUBK", [128, 128], F32)
            LAMT = sb("LAMT", [128, 8], F32)
            BIASB = sb("BIASB", [128, 4 * 16 * 16], F32)
            LG = sb("LG", [128, 6, 64], F32)
            ONE16 = sb("ONE16", [128, 16], F32)
            P0 = ps("P0", [128, 512], F32)
            P1 = ps("P1", [128, 512], F32)
            P2 = ps("P2", [128, 512], F32)
            P3 = ps("P3", [128, 512], F32)
            P4 = ps("P4", [128, 512], F32)
            P5 = ps("P5", [128, 512], F32)
            T0 = ps("T0", [128, 1024], BF16)
            P6 = ps("P6", [128, 512], F32)
            self.BH, self.BB, self.WS, self.FS, self.HB, self.PT = BH, BB, WS, FS, HB, PT
            self.OTL, self.N1, self.DT, self.JK, self.SCL = OTL, N1, DT, JK, SCL
            self.JK2 = JK2
            self.MASK, self.ALIBI, self.CMB, self.CMF, self.PV = MASK, ALIBI, CMB, CMF, PV
            self.GSUBK, self.LAMT, self.BIASB, self.LG, self.ONE16 = GSUBK, LAMT, BIASB, LG, ONE16
            self.P = [P0, P1, P2, P3, P4, P5, P6]
            self.T = [T0, T0]
            self.init_state()
            self.record()
            self.sc.finalize()
            self.emit_all()
        return nc

    def init_state(self):
        sc = self.sc
        self.st_w = [sc.new_stream() for _ in range(NW)]
        self.st_f = [sc.new_stream() for _ in range(NF)]
        self.st_hb = [sc.new_stream() for _ in range(NHB)]
        self.st_misc = sc.new_stream()
        self.st_bb = sc.new_stream()
        self.st_tap = sc.new_stream()
        self.wi = 0
        self.fi = 0
        self.hi = 0
        self.pti = 0
        self.ptwi = 0
        self.sci = 0
        self.zi = 0
        self.qkn = 0
        self.conv_ops = {}
        self.prev_conv = None
        self.all_conv = []

    def fslot(self):
        i = self.fi
        self.fi = (i + 1) % NF
        return i

    def hslot(self):
        i = self.hi
        self.hi = (i + 1) % NHB
        return i

    def ptslot(self):
        i = self.pti
        self.pti = (i + 1) % NPT
        return i

    def ptwslot(self):
        i = self.ptwi
        self.ptwi = (i + 1) % 3
        return i

    def scol(self):
        i = self.sci
        self.sci = (i + 1) % 64
        return i

    def zbank(self):
        i = self.zi
        self.zi = (i + 1) % 2
        return i

    def convert_layer(self, L):
        sc = self.sc

        def conv(name, src, dst, nrows, rows_per):
            st = sc.new_stream()
            ops = []
            KT = 6
            for r0 in range(0, nrows, rows_per):
                s_ap = src[L][r0:r0 + rows_per, :]
                d_ap = dst[L][r0:r0 + rows_per, :]
                o = sc.add("pool", lambda e, s_ap=s_ap, d_ap=d_ap: e.dma_start(out=d_ap, in_=s_ap, max_dma_last_dim=8192),
                           r=[], w=[("wb", L, name, r0)], dma=st)
                xw = []
                if not ops and self.prev_conv is not None:
                    xw.append(self.prev_conv)
                if len(ops) >= KT:
                    xw.append((st, 16 * (len(ops) - KT + 1)))
                o.xw = xw
                ops.append(o)
            fin = 16 * sc.streams[st]
            self.prev_conv = (st, fin)
            self.all_conv.append((st, fin))
            for o in ops:
                o.val = fin
            self.conv_ops[(L, name)] = ops

        import os as _os
        sel = _os.environ.get("KCONV", "")
        if sel:
            tbl = {"in": (self.w_in, self.wb_in, D, 128), "up_a": (self.w_up_a, self.wb_up_a, 512, 512),
                   "up_c": (self.w_up_c, self.wb_up_c, 768, 768), "out": (self.w_out, self.wb_out, D, 512),
                   "ffi": (self.w_ffi, self.wb_ffi, D, 128), "ffo": (self.w_ffo, self.wb_ffo, 4 * D, 512)}
            for nm in sel.split(","):
                conv(nm, *tbl[nm])
            return
        conv("in", self.w_in, self.wb_in, D, 128)
        conv("up_a", self.w_up_a, self.wb_up_a, 512, 512)
        conv("up_b", self.w_up_b, self.wb_up_b, 512, 512)
        conv("up_c", self.w_up_c, self.wb_up_c, 768, 768)
        conv("out", self.w_out, self.wb_out, D, 512)
        conv("ffi", self.w_ffi, self.wb_ffi, D, 128)
        conv("ffo", self.w_ffo, self.wb_ffo, 4 * D, 512)

    def wdeps(self, L, name):
        return self.conv_ops.get((L, name), [])

    def wfill(self, L, name, dram2d, r0, nk, c0, ncols, slot=None, k0=0, part=None):
        sc = self.sc
        if slot is None:
            slot = self.wi
            self.wi = (slot + 1) % NW
        src = dram2d[r0:r0 + nk * 128, c0:c0 + ncols].rearrange("(c p) n -> p c n", p=128)
        dst = self.WS[:, slot, k0:k0 + nk, 0:ncols]
        wk = [("W", slot, p) for p in ((0, 1, 2) if part is None else (part,))]
        op = sc.add("sp", lambda e: e.dma_start(out=dst, in_=src), r=[], w=wk, dma=self.st_w[slot])
        for d in self.wdeps(L, name):
            if all(d is not x for x in op.deps):
                op.deps.append(d)
        self.last_fill_op = op
        return slot

    def wkeys(self, slot):
        return [("W", slot, 0), ("W", slot, 1), ("W", slot, 2)]

    def record(self):
        sc = self.sc
        import os as _os
        if int(_os.environ.get("KNOMISC", "0")):
            for L in range(self.n_layers):
                self.convert_layer(L)
            fo = sc.add("pool", None)
            fo.xw = list(self.all_conv)
            return
        cops = []
        for dst, src, key in ((self.MASK, self.cmask, "MASK"), (self.ALIBI, self.calibi, "ALIBI"),
                              (self.CMB, self.cmatb, "CMB"), (self.CMF, self.cmatf, "CMF")):
            cops.append(sc.add("sp", lambda e, dst=dst, src=src: e.dma_start(out=dst[:, :], in_=src[:, :]), w=[key], dma=self.st_misc))
        batch_final(sc, cops)
        sc.add("dve", lambda e: e.memset(self.ONE16[:, :], 1.0), w=["ONE16"])
        import os as _os
        self.noconv = bool(int(_os.environ.get("KNOCONV", "0")))
        for L in range(self.n_layers):
            if not self.noconv:
                self.convert_layer(L)
        for L in range(self.n_layers):
            self.layer_params(L)
            for sq in range(self.n_seq):
                if self.stop <= 0:
                    continue
                self.rms_phase(L, sq, first=(L == 0), gbase=P_GMIX)
                self.tap("hT", self.BH[:, :, :], [("H", c, t) for c in range(16) for t in range(4)])
                if self.stop <= 1:
                    continue
                self.attention_phase(L, sq)
                if "oT" in self.tap_out and "oT" not in self.tap_list:
                    self.tap_list.append("oT")
                    ks = [k for k in self.sc.lastw if isinstance(k, tuple) and k[0] == "oTsp"]
                    self.sc.add("sp", lambda e: e.dma_start(out=self.tap_out["oT"], in_=self.oT_sp[0]), r=ks, w=[("tap", "oT")], dma=self.st_tap)
                if self.stop <= 4:
                    continue
                self.merge_phase(L, sq)
                if "mT" in self.tap_out and "mT" not in self.tap_list:
                    self.tap_list.append("mT")
                    ks = [k for k in self.sc.lastw if isinstance(k, tuple) and k[0] == "mTsp"]
                    self.sc.add("sp", lambda e: e.dma_start(out=self.tap_out["mT"], in_=self.mT_sp[0]), r=ks, w=[("tap", "mT")], dma=self.st_tap)
                if self.stop <= 5:
                    continue
                self.wout_phase(L, sq, first=(L == 0))
                if self.stop <= 6:
                    continue
                self.rms_phase(L, sq, first=False, gbase=P_GFFN)
                if self.stop <= 7:
                    continue
                self.ffn_phase(L, sq)
        fo = sc.add("pool", None)
        fo.xw = list(self.all_conv)
        keys = [("yT", sq, m, tg) for sq in range(self.n_seq) for m in range(16) for tg in range(4)]
        sc.add("sp", None, r=keys)
        if self.tap_list:
            sc.add("sp", None, r=[("tap", n) for n in self.tap_list])

    def tap(self, name, ap, keys):
        if name not in self.tap_out or name in self.tap_list:
            return
        self.tap_list.append(name)
        dst = self.tap_out[name]
        if len(dst.shape) == 2 and len(ap.shape) == 3:
            dst = dst.rearrange("p (a b) -> p a b", a=ap.shape[1])
        self.sc.add("sp", lambda e: e.dma_start(out=dst, in_=ap), r=keys, w=[("tap", name)], dma=self.st_tap)

    def layer_params(self, L):
        sc = self.sc
        PV = self.PV
        lam_init = 0.8 - 0.6 * math.exp(-0.3 * L)
        sc.add("sp", lambda e: e.dma_start(out=PV[:, :], in_=self.pvec[L]), w=["PV"], dma=self.st_misc)
        LT = self.LAMT
        JK = self.JK
        sc.add("dve", lambda e: e.tensor_tensor(out=JK[:, 0:64], in0=PV[:, P_LQ1:P_LQ1 + 64], in1=PV[:, P_LK1:P_LK1 + 64], op=ALU.mult),
               r=["PV"], w=["JK"])
        sc.add("dve", lambda e: e.reduce_sum(out=LT[:, 0:1], in_=JK[:, 0:64], axis=AXX), r=["JK"], w=[("LT", 0)])
        sc.add("dve", lambda e: e.tensor_tensor(out=JK[:, 64:128], in0=PV[:, P_LQ2:P_LQ2 + 64], in1=PV[:, P_LK2:P_LK2 + 64], op=ALU.mult),
               r=["PV"], w=["JK2"])
        sc.add("dve", lambda e: e.reduce_sum(out=LT[:, 1:2], in_=JK[:, 64:128], axis=AXX), r=["JK2"], w=[("LT", 1)])
        sc.add("act", lambda e: e.activation(out=LT[:, 2:4], in_=LT[:, 0:2], func=AF.Exp), r=[("LT", 0), ("LT", 1)], w=[("LT", 2)])
        sc.add("dve", lambda e: e.tensor_tensor(out=LT[:, 4:5], in0=LT[:, 3:4], in1=LT[:, 2:3], op=ALU.subtract), r=[("LT", 2)], w=[("LT", 4)])
        sc.add("dve", lambda e: e.tensor_scalar(out=LT[:, 5:6], in0=LT[:, 4:5], scalar1=-lam_init, scalar2=None, op0=ALU.add),
               r=[("LT", 4)], w=["NEGLAM"])
        kk = math.sqrt(128.0) * (1.0 - lam_init)
        sc.add("dve", lambda e: e.tensor_scalar(out=self.GSUBK[:, :], in0=PV[:, P_GSUB:P_GSUB + 128], scalar1=kk, scalar2=None, op0=ALU.mult),
               r=["PV"], w=["GSUBK"])

    def xsrc(self, first, sq, c, tg):
        t = self.xT if first else self.yT
        return t[sq][c * 128:(c + 1) * 128, tg * 512:(tg + 1) * 512]

    def xkey(self, first, sq, c, tg):
        return ("xT" if first else "yT", sq, c, tg)

    def rms_phase(self, L, sq, first, gbase):
        sc = self.sc
        FS, HB, BH, PV = self.FS, self.HB, self.BH, self.PV
        ones_d = self.CMB[:, 128:256]
        for tg in range(4):
            ssb = 2 + (tg % 2)
            Pss = self.P[ssb]
            for c in range(16):
                fs = self.fslot()
                src = self.xsrc(first, sq, c, tg)
                sc.add("sp", lambda e, fs=fs, src=src: e.dma_start(out=FS[:, fs, :], in_=src),
                       r=[self.xkey(first, sq, c, tg)], w=[("F", fs)], dma=self.st_f[fs])
                hs = self.hslot()
                sc.add("act", lambda e, fs=fs, hs=hs: e.activation(out=HB[:, hs, :], in_=FS[:, fs, :], func=AF.Square),
                       r=[("F", fs)], w=[("HB", hs)])
                sc.add("pe", lambda e, hs=hs, c=c, Pss=Pss: e.matmul(Pss[:, :], lhsT=ones_d, rhs=HB[:, hs, :], start=(c == 0), stop=(c == 15)),
                       r=[("HB", hs), "CMB"], w=[("P", ssb)])
            rs = self.fslot()
            sc.add("act", lambda e, rs=rs, Pss=Pss: e.activation(out=FS[:, rs, :], in_=Pss[:, :], func=AF.Ln, bias=EPS),
                   r=[("P", ssb)], w=[("F", rs)])
            sc.add("act", lambda e, rs=rs: e.activation(out=FS[:, rs, :], in_=FS[:, rs, :], func=AF.Exp, scale=-0.5),
                   r=[("F", rs)], w=[("F", rs)])
            for c in range(16):
                fs = self.fslot()
                if fs == rs:
                    fs = self.fslot()
                src = self.xsrc(first, sq, c, tg)
                sc.add("sp", lambda e, fs=fs, src=src: e.dma_start(out=FS[:, fs, :], in_=src),
                       r=[self.xkey(first, sq, c, tg)], w=[("F", fs)], dma=self.st_f[fs])
                sc.add("dve", lambda e, fs=fs, rs=rs, c=c, tg=tg: e.scalar_tensor_tensor(
                    out=BH[:, c, tg * 512:(tg + 1) * 512], in0=FS[:, fs, :], scalar=PV[:, gbase + c:gbase + c + 1],
                    in1=FS[:, rs, :], op0=ALU.mult, op1=ALU.mult),
                    r=[("F", fs), ("F", rs), "PV"], w=[("H", c, tg)])

    def attention_phase(self, L, sq):
        groups = [
            ("A", 4, C_QA, C_KA, C_VA, 0),
            ("B", 4, C_QB, C_KB, C_VB, 4),
            ("C", 3, C_QC, C_KC, C_VC, 8),
            ("C", 3, C_QC + 384, C_KC + 384, C_VC + 384, 11),
        ]
        for gi, (kind, nh, qcol, kcol, vcol, oc0) in enumerate(groups):
            self.project_group(L, sq, kind, nh, qcol, kcol, vcol)
            if kind == "B":
                self.forget_bias(L, sq)
            if gi == self.tap_group and L == 0 and sq == 0:
                self.tap("BB", self.BB[:, :], bkeys(0, 32768))
            if self.stop <= 2:
                return
            self.attend_group(L, sq, kind, nh, oc0, c_head0=(3 if gi == 3 else 0))
            if self.stop <= 3 and gi == self.tap_group:
                return

    def project_group(self, L, sq, kind, nh, qcol, kcol, vcol):
        sc = self.sc
        BB, BH, FS, HB, PV, WS = self.BB, self.BH, self.FS, self.HB, self.PV, self.WS
        wb = self.wb_in[L]
        if kind == "A":
            onesm = self.CMB[:, 384:512]
            gq, gk = P_GQA, P_GKA
        elif kind == "B":
            onesm = self.CMB[:, 256:384]
            gq, gk = P_GQB, P_GKB
        else:
            onesm = self.CMB[:, 256:384]
            gq, gk = P_GQC, P_GKC
        pending = []

        def flush():
            while pending:
                pending.pop(0)()

        for (col0, dst0, gcol) in ((qcol, 0, gq), (kcol, 4, gk)):
            ci = 0
            while ci < nh:
                ncn = min(2, nh - ci)
                slot = self.wfill(L, "in", wb, 0, 16, col0 + ci * 128, ncn * 128)
                for mm in range(ncn):
                    ch = dst0 + ci + mm
                    for tg in range(4):
                        zb = (0, 1, 4, 5)[self.qkn % 4]
                        ssb_n = 2 + (self.qkn % 2)
                        self.qkn += 1
                        Pz = self.P[zb]
                        for k in range(16):
                            sc.add("pe", lambda e, Pz=Pz, slot=slot, k=k, mm=mm, tg=tg: e.matmul(
                                Pz[:, :], lhsT=WS[:, slot, k, mm * 128:(mm + 1) * 128], rhs=BH[:, k, tg * 512:(tg + 1) * 512],
                                start=(k == 0), stop=(k == 15)),
                                r=self.wkeys(slot) + [("H", k, tg)], w=[("P", zb)])
                        hs = self.hslot()
                        sc.add("act", lambda e, Pz=Pz, hs=hs: e.activation(out=HB[:, hs, :], in_=Pz[:, :], func=AF.Square),
                               r=[("P", zb)], w=[("HB", hs)])
                        flush()

                        def tail(zb=zb, Pz=Pz, hs=hs, ch=ch, tg=tg, gcol=gcol, ssb=ssb_n):
                            Pss = self.P[ssb]
                            sc.add("pe", lambda e: e.matmul(Pss[:, :], lhsT=onesm, rhs=HB[:, hs, :], start=True, stop=True),
                                   r=[("HB", hs), "CMB"], w=[("P", ssb)])
                            rs = self.fslot()
                            sc.add("act", lambda e: e.activation(out=FS[:, rs, :], in_=Pss[:, :], func=AF.Ln, bias=EPS),
                                   r=[("P", ssb)], w=[("F", rs)])
                            sc.add("act", lambda e: e.activation(out=FS[:, rs, :], in_=FS[:, rs, :], func=AF.Exp, scale=-0.5),
                                   r=[("F", rs)], w=[("F", rs)])
                            lo = ch * 2048 + tg * 512
                            sc.add("dve", lambda e: e.scalar_tensor_tensor(
                                out=BB[:, lo:lo + 512], in0=Pz[:, :], scalar=PV[:, gcol:gcol + 1], in1=FS[:, rs, :],
                                op0=ALU.mult, op1=ALU.mult),
                                r=[("P", zb), ("F", rs), "PV"], w=bkeys(lo, lo + 512))
                        pending.append(tail)
                ci += ncn
        flush()
        vreg = BB[:, VOFF:VOFF + 64 * VST].rearrange("p (n e) -> p n e", e=VST)
        sc.add("dve", lambda e: e.memset(vreg[:, :, 128:129], 1.0), w=bkeys(VOFF, VOFF + 64 * VST))
        ci = 0
        evi = 0
        while ci < nh:
            ncn = min(2, nh - ci)
            slot = self.wfill(L, "in", wb, 0, 16, vcol + ci * 128, ncn * 128)
            for tt in range(16):
                zb = self.zbank()
                Pz = self.P[zb]
                for k in range(16):
                    sc.add("pe", lambda e, Pz=Pz, slot=slot, k=k, tt=tt, ncn=ncn: e.matmul(
                        Pz[:, 0:ncn * 128], lhsT=BH[:, k, tt * 128:(tt + 1) * 128], rhs=WS[:, slot, k, 0:ncn * 128],
                        start=(k == 0), stop=(k == 15)),
                        r=self.wkeys(slot) + [("H", k, tt // 4)], w=[("P", zb)])
                lo = VOFF + (tt * 4 + ci) * VST
                dst = BB[:, lo:lo + ncn * VST].rearrange("p (n e) -> p n e", e=VST)[:, :, 0:128]
                srcp = Pz[:, 0:ncn * 128].rearrange("p (n e) -> p n e", e=128)
                if evi % 2 == 0:
                    sc.add("act", lambda e, dst=dst, srcp=srcp: e.activation(out=dst, in_=srcp, func=AF.Copy),
                           r=[("P", zb)], w=bkeys(lo, lo + ncn * VST))
                else:
                    sc.add("dve", lambda e, dst=dst, srcp=srcp: e.tensor_copy(out=dst, in_=srcp),
                           r=[("P", zb)], w=bkeys(lo, lo + ncn * VST))
                evi += 1
            ci += ncn

    def forget_bias(self, L, sq):
        sc = self.sc
        BH, WS, PV, LG = self.BH, self.WS, self.PV, self.LG
        wb = self.wb_in[L]
        slot = self.wfill(L, "in", wb, 0, 16, C_FB, 4)
        Pf = self.P[5]
        for tt in range(16):
            for k in range(16):
                sc.add("pe", lambda e, k=k, tt=tt: e.matmul(
                    Pf[:, tt * 4:(tt + 1) * 4], lhsT=BH[:, k, tt * 128:(tt + 1) * 128], rhs=WS[:, slot, k, 0:4],
                    start=(k == 0), stop=(k == 15)),
                    r=self.wkeys(slot) + [("H", k, tt // 4)], w=[("P", 5)])
        sc.add("dve", lambda e: e.tensor_tensor(out=LG[:, 0, :], in0=Pf[:, 0:64], in1=PV[:, P_BF:P_BF + 64], op=ALU.add),
               r=[("P", 5), "PV"], w=[("LG", 0)])
        sc.add("act", lambda e: e.activation(out=LG[:, 1, :], in_=LG[:, 0, :], func=AF.Exp, scale=-1.0), r=[("LG", 0)], w=[("LG", 1)])
        sc.add("act", lambda e: e.activation(out=LG[:, 2, :], in_=LG[:, 1, :], func=AF.Ln, bias=1.0), r=[("LG", 1)], w=[("LG", 2)])
        U = self.CMF[:, 0:128]
        ONES = self.CMF[:, 128:256]
        Pc = self.P[4]
        sc.add("pe", lambda e: e.matmul(Pc[:, 0:64], lhsT=U, rhs=LG[:, 2, :], start=True, stop=True), r=[("LG", 2), "CMF"], w=[("P", 4)])
        sc.add("pe", lambda e: e.matmul(Pc[:, 64:128], lhsT=ONES, rhs=LG[:, 2, :], start=True, stop=True), r=[("LG", 2), "CMF"], w=[("P", 4)])
        sc.add("dve", lambda e: e.tensor_copy(out=LG[:, 3, :], in_=Pc[:, 64:128]), r=[("P", 4)], w=[("LG", 3)])
        for h in range(4):
            sc.add("dve", lambda e, h=h: e.tensor_tensor_scan(out=LG[:, 4, h:64:4], data0=self.ONE16[:, :], data1=LG[:, 3, h:64:4],
                                                               initial=0.0, op0=ALU.mult, op1=ALU.add),
                   r=[("LG", 3), "ONE16"], w=[("LG", 4, h)])
        i4 = [("LG", 4, h) for h in range(4)]
        sc.add("dve", lambda e: e.tensor_tensor(out=LG[:, 5, :], in0=Pc[:, 0:64], in1=LG[:, 4, :], op=ALU.add), r=[("P", 4)] + i4, w=[("LG", 5)])
        sc.add("dve", lambda e: e.tensor_tensor(out=LG[:, 5, :], in0=LG[:, 5, :], in1=LG[:, 3, :], op=ALU.subtract),
               r=[("LG", 5), ("LG", 3)], w=[("LG", 5)])
        BI = self.BIASB
        for h in range(4):
            for i in range(16):
                o0 = (h * 16 + i) * 16
                sc.add("dve", lambda e, h=h, i=i, o0=o0: e.tensor_scalar(
                    out=BI[:, o0:o0 + i + 1], in0=LG[:, 5, h:h + 4 * i + 1:4], scalar1=LG[:, 4, i * 4 + h:i * 4 + h + 1], scalar2=None,
                    op0=ALU.subtract),
                    r=[("LG", 5)] + i4, w=[("BIASB", h, i)])
        self.tap("biasB", BI[:, :], [("BIASB", h, i) for h in range(4) for i in range(16)])
        self.tap("lg", LG[:, :, :], [("LG", 5), ("LG", 3)] + i4)

    def attend_group(self, L, sq, kind, nh, oc0, c_head0=0):
        sc = self.sc
        HB = self.HB
        for h in range(nh):
            units = [(0, 64), (64, 64)] if kind == "A" else [(0, 128)]
            for G in range(4):
                hs_out = self.hslot()
                for ui, (pb, dk) in enumerate(units):
                    self.attend_unit(kind, h, G, ui, pb, dk, c_head0, hs_out)
                oc = oc0 + h
                dst = self.oT_sp[sq][oc * 128:(oc + 1) * 128, G * 512:(G + 1) * 512]
                sc.add("act", lambda e, dst=dst, hs_out=hs_out: e.dma_start(out=dst, in_=HB[:, hs_out, :]),
                       r=[("HB", hs_out)], w=[("oTsp", sq, oc, G)], dma=self.st_hb[hs_out])

    def attend_unit(self, kind, h, G, ui, pb, dk, c_head0, hs_out):
        sc = self.sc
        BB, PT, MASK, ALIBI, BI = self.BB, self.PT, self.MASK, self.ALIBI, self.BIASB
        OTL, N1, DT, SCL, HB = self.OTL, self.N1, self.DT, self.SCL, self.HB
        causal = MASK[:, 0:128]
        scale = 0.125 if kind == "A" else 128.0 ** -0.5
        qrows = BB[pb:pb + dk, h * 2048:(h + 1) * 2048]
        krows = BB[pb:pb + dk, (4 + h) * 2048:(5 + h) * 2048]
        qk = lambda lo, hi: bkeys(h * 2048 + lo, h * 2048 + hi)
        kk = lambda lo, hi: bkeys((4 + h) * 2048 + lo, (4 + h) * 2048 + hi)
        if kind == "A":
            slope = 2.0 ** (-8.0 * (h + 1) / 4)
        elif kind == "C":
            slope = 2.0 ** (-8.0 * (c_head0 + h + 1) / 6)
        else:
            slope = None
        wide = slope is not None and slope * 511.0 <= 40.0
        omax = 15
        if slope is not None and not wide:
            omax = min(15, int(math.ceil(160.0 / (slope * 128.0))) - 1)
        PTW = self.PTW
        j_min = max(0, 4 * G - omax)
        js = list(range(j_min, 4 * G + 4))

        def cols(j):
            i_lo = max(j, 4 * G)
            i_hi = min(4 * G + 3, j + omax)
            return i_lo, i_hi

        SB = (0, 1, 6)

        def s_mm(j):
            i_lo, i_hi = cols(j)
            sb = SB[j % 3]
            Ps = self.P[sb]
            c0 = (i_lo - 4 * G) * 128
            c1 = (i_hi + 1 - 4 * G) * 128
            sc.add("pe", lambda e: e.matmul(Ps[:, c0:c1], lhsT=krows[:, j * 128:(j + 1) * 128],
                                            rhs=qrows[:, i_lo * 128:(i_hi + 1) * 128], start=True, stop=True),
                   r=kk(j * 128, (j + 1) * 128) + qk(i_lo * 128, (i_hi + 1) * 128), w=[("P", sb)])

        def pv(j):
            i_lo, i_hi = cols(j)
            sb = SB[j % 3]
            Ps = self.P[sb]
            vlo = VOFF + (j * 4 + h) * VST
            if wide:
                c0 = (i_lo - 4 * G) * 128
                op_ = 4 * G + 3 - j
                if kind == "A":
                    bias = ALIBI[:, h * 16 + op_:h * 16 + op_ + 1]
                else:
                    hh = c_head0 + h
                    bias = ALIBI[:, 64 + hh * 16 + op_:64 + hh * 16 + op_ + 1]
                ws = self.ptwslot()
                sc.add("act", lambda e: e.activation(out=PTW[:, ws, c0:512], in_=Ps[:, c0:512], func=AF.Exp, bias=bias, scale=scale),
                       r=[("P", sb), "ALIBI"], w=[("PTW", ws)])
                if kind == "A":
                    if j >= 4 * G:
                        sc.add("dve", lambda e: e.tensor_tensor(out=PTW[:, ws, c0:c0 + 128], in0=PTW[:, ws, c0:c0 + 128], in1=causal, op=ALU.mult),
                               r=[("PTW", ws), "MASK"], w=[("PTW", ws)])
                else:
                    o0 = i_lo - j
                    mk = MASK[:, (1 + o0) * 128:(1 + o0) * 128 + (512 - c0)]
                    sc.add("dve", lambda e: e.tensor_tensor(out=PTW[:, ws, c0:512], in0=PTW[:, ws, c0:512], in1=mk, op=ALU.mult),
                           r=[("PTW", ws), "MASK"], w=[("PTW", ws)])
                for i in range(i_lo, i_hi + 1):
                    qi = i - 4 * G
                    Po = self.P[2 + qi]
                    sc.add("pe", lambda e, qi=qi, Po=Po, i=i: e.matmul(
                        Po[:, 0:129], lhsT=PTW[:, ws, qi * 128:(qi + 1) * 128], rhs=BB[:, vlo:vlo + 129],
                        start=(j == max(0, i - omax)), stop=(j == i)),
                        r=[("PTW", ws)] + bkeys(vlo, vlo + 129), w=[("P", 2 + qi)])
                return
            for i in range(i_lo, i_hi + 1):
                qi = i - 4 * G
                o = i - j
                if kind == "A":
                    bias = ALIBI[:, h * 16 + o:h * 16 + o + 1]
                    bkey = ["ALIBI"]
                    mask = causal if o == 0 else None
                elif kind == "B":
                    off = (h * 16 + i) * 16 + j
                    bias = BI[:, off:off + 1]
                    bkey = [("BIASB", h, i)]
                    mask = causal if o == 0 else None
                else:
                    hh = c_head0 + h
                    bias = ALIBI[:, 64 + hh * 16 + o:64 + hh * 16 + o + 1]
                    bkey = ["ALIBI"]
                    mask = MASK[:, (1 + o) * 128:(2 + o) * 128]
                ps = self.ptslot()
                sc.add("act", lambda e, ps=ps, qi=qi, bias=bias: e.activation(
                    out=PT[:, ps, :], in_=Ps[:, qi * 128:(qi + 1) * 128], func=AF.Exp, bias=bias, scale=scale),
                    r=[("P", sb)] + bkey, w=[("PT", ps)])
                if mask is not None:
                    sc.add("dve", lambda e, ps=ps, mask=mask: e.tensor_tensor(out=PT[:, ps, :], in0=PT[:, ps, :], in1=mask, op=ALU.mult),
                           r=[("PT", ps), "MASK"], w=[("PT", ps)])
                Po = self.P[2 + qi]
                sc.add("pe", lambda e, ps=ps, Po=Po, i=i: e.matmul(
                    Po[:, 0:129], lhsT=PT[:, ps, :], rhs=BB[:, vlo:vlo + 129], start=(j == max(0, i - omax)), stop=(j == i)),
                    r=[("PT", ps)] + bkeys(vlo, vlo + 129), w=[("P", 2 + qi)])

        s_mm(js[0])
        if len(js) > 1:
            s_mm(js[1])
        for n, j in enumerate(js):
            if n + 2 < len(js):
                s_mm(js[n + 2])
            pv(j)
        steps = [[] for _ in range(4)]
        for qi in range(4):
            Po = self.P[2 + qi]
            pk = ("P", 2 + qi)
            st = steps[qi]
            if kind == "A" and ui == 0:
                c1 = self.scol()
                st.append(lambda Po=Po, c1=c1, pk=pk: sc.add("dve", lambda e: e.reciprocal(out=SCL[:, c1:c1 + 1], in_=Po[:, 128:129]), r=[pk], w=[("SC", c1)]))
                st.append(lambda Po=Po, c1=c1, qi=qi, pk=pk: sc.add("dve", lambda e: e.tensor_scalar(
                    out=N1[:, qi, :], in0=Po[:, 0:128], scalar1=SCL[:, c1:c1 + 1], scalar2=None, op0=ALU.mult),
                    r=[pk, ("SC", c1)], w=[("N1", qi)]))
                continue
            if kind == "A":
                c1, c2, c3, c4, c5 = (self.scol() for _ in range(5))
                st.append(lambda Po=Po, c1=c1, pk=pk: sc.add("dve", lambda e: e.reciprocal(out=SCL[:, c1:c1 + 1], in_=Po[:, 128:129]), r=[pk], w=[("SC", c1)]))
                st.append(lambda c1=c1, c2=c2: sc.add("dve", lambda e: e.tensor_tensor(out=SCL[:, c2:c2 + 1], in0=SCL[:, c1:c1 + 1], in1=self.LAMT[:, 5:6], op=ALU.mult),
                                                      r=[("SC", c1), "NEGLAM"], w=[("SC", c2)]))
                st.append(lambda Po=Po, c2=c2, qi=qi, pk=pk: sc.add("dve", lambda e: e.scalar_tensor_tensor(
                    out=DT[:, qi, :], in0=Po[:, 0:128], scalar=SCL[:, c2:c2 + 1], in1=N1[:, qi, :], op0=ALU.mult, op1=ALU.add),
                    r=[pk, ("SC", c2), ("N1", qi)], w=[("DT", qi)]))
                st.append(lambda qi=qi, c3=c3: sc.add("act", lambda e: e.activation(out=self.JK2[:, :], in_=DT[:, qi, :], func=AF.Square, accum_out=SCL[:, c3:c3 + 1]),
                                                      r=[("DT", qi)], w=[("SC", c3), "JK2"]))
                st.append(lambda c3=c3, c4=c4: sc.add("act", lambda e: e.activation(out=SCL[:, c4:c4 + 1], in_=SCL[:, c3:c3 + 1], func=AF.Ln, bias=128.0 * EPS),
                                                      r=[("SC", c3)], w=[("SC", c4)]))
                st.append(lambda c4=c4, c5=c5: sc.add("act", lambda e: e.activation(out=SCL[:, c5:c5 + 1], in_=SCL[:, c4:c4 + 1], func=AF.Exp, scale=-0.5),
                                                      r=[("SC", c4)], w=[("SC", c5)]))
                st.append(lambda qi=qi, c5=c5: sc.add("dve", lambda e: e.scalar_tensor_tensor(
                    out=OTL[:, qi, :], in0=DT[:, qi, :], scalar=SCL[:, c5:c5 + 1], in1=self.GSUBK[:, :], op0=ALU.mult, op1=ALU.mult),
                    r=[("DT", qi), ("SC", c5), "GSUBK"], w=[("OTL", qi)]))
            else:
                c1 = self.scol()
                st.append(lambda Po=Po, c1=c1, pk=pk: sc.add("dve", lambda e: e.reciprocal(out=SCL[:, c1:c1 + 1], in_=Po[:, 128:129]), r=[pk], w=[("SC", c1)]))
                st.append(lambda Po=Po, c1=c1, qi=qi, pk=pk: sc.add("dve", lambda e: e.tensor_scalar(
                    out=OTL[:, qi, :], in0=Po[:, 0:128], scalar1=SCL[:, c1:c1 + 1], scalar2=None, op0=ALU.mult),
                    r=[pk, ("SC", c1)], w=[("OTL", qi)]))
        for si in range(max(len(x) for x in steps)):
            for qi in range(4):
                if si < len(steps[qi]):
                    steps[qi][si]()
        if kind == "A" and ui == 0:
            return
        Tb = self.T[0]
        for qi in range(4):
            sc.add("pe", lambda e, qi=qi: e.transpose(out=Tb[:, qi * 128:(qi + 1) * 128], in_=OTL[:, qi, :], identity=self.CMB[:, 0:128]),
                   r=[("OTL", qi), "CMB"], w=[("T", 0)])
        sc.add("dve", lambda e: e.tensor_copy(out=HB[:, hs_out, :], in_=Tb[:, 0:512]), r=[("T", 0)], w=[("HB", hs_out)])

    def merge_phase(self, L, sq):
        sc = self.sc
        BB, BH, FS, HB, PV, WS = self.BB, self.BH, self.FS, self.HB, self.PV, self.WS
        lops = []
        for c in range(14):
            src = self.oT_sp[sq][c * 128:(c + 1) * 128, :]
            lops.append(sc.add("sp", lambda e, c=c, src=src: e.dma_start(out=BB[:, c * 2048:(c + 1) * 2048], in_=src),
                               r=[("oTsp", sq, c, G) for G in range(4)], w=bkeys(c * 2048, (c + 1) * 2048), dma=self.st_bb))
        batch_final(sc, lops)
        branches = [(0, 4, self.wb_up_a, "up_a"), (4, 4, self.wb_up_b, "up_b"), (8, 6, self.wb_up_c, "up_c")]
        for fp in range(8):
            gslots = []
            for b in range(3):
                gslots.append(self.wfill(L, "in", self.wb_in[L], 0, 16, C_GL + b * 2048 + fp * 256, 256))
            us = self.wi
            self.wi = (us + 1) % NW
            fops = []
            for b, (k0, nk, wbt, nm) in enumerate(branches):
                self.wfill(L, nm, wbt[L], 0, nk, fp * 256, 256, slot=us, k0=k0, part=b)
                fops.append(self.last_fill_op)
            batch_final(sc, fops)
            for mm in range(2):
                f = fp * 2 + mm
                for tg in range(4):
                    acc = None
                    for b, (k0, nk, wbt, nm) in enumerate(branches):
                        zb = self.zbank()
                        Pz = self.P[zb]
                        gs = gslots[b]
                        for k in range(16):
                            sc.add("pe", lambda e, Pz=Pz, gs=gs, k=k, mm=mm, tg=tg: e.matmul(
                                Pz[:, :], lhsT=WS[:, gs, k, mm * 128:(mm + 1) * 128], rhs=BH[:, k, tg * 512:(tg + 1) * 512],
                                start=(k == 0), stop=(k == 15)),
                                r=self.wkeys(gs) + [("H", k, tg)], w=[("P", zb)])
                        gf = self.fslot()
                        bcol = P_BGATE + b * 16 + f
                        sc.add("act", lambda e, Pz=Pz, gf=gf, bcol=bcol: e.activation(
                            out=FS[:, gf, :], in_=Pz[:, :], func=AF.Sigmoid, bias=PV[:, bcol:bcol + 1]),
                            r=[("P", zb), "PV"], w=[("F", gf)])
                        ub = 2 + zb
                        Pu = self.P[ub]
                        for kk in range(nk):
                            lo = (k0 + kk) * 2048 + tg * 512
                            sc.add("pe", lambda e, Pu=Pu, kk=kk, lo=lo, k0=k0, nk=nk, us=us, mm=mm: e.matmul(
                                Pu[:, :], lhsT=WS[:, us, k0 + kk, mm * 128:(mm + 1) * 128], rhs=BB[:, lo:lo + 512],
                                start=(kk == 0), stop=(kk == nk - 1)),
                                r=[("W", us, b)] + bkeys(lo, lo + 512), w=[("P", ub)])
                        if b == 0:
                            acc = self.fslot()
                            sc.add("dve", lambda e, Pu=Pu, gf=gf, acc=acc: e.tensor_tensor(out=FS[:, acc, :], in0=Pu[:, :], in1=FS[:, gf, :], op=ALU.mult),
                                   r=[("P", ub), ("F", gf)], w=[("F", acc)])
                        else:
                            sc.add("dve", lambda e, Pu=Pu, gf=gf: e.tensor_tensor(out=FS[:, gf, :], in0=Pu[:, :], in1=FS[:, gf, :], op=ALU.mult),
                                   r=[("P", ub), ("F", gf)], w=[("F", gf)])
                            if b == 1:
                                sc.add("dve", lambda e, gf=gf, acc=acc: e.tensor_tensor(out=FS[:, acc, :], in0=FS[:, acc, :], in1=FS[:, gf, :], op=ALU.add),
                                       r=[("F", gf), ("F", acc)], w=[("F", acc)])
                            else:
                                hs = self.hslot()
                                sc.add("dve", lambda e, gf=gf, acc=acc, hs=hs: e.tensor_tensor(out=HB[:, hs, :], in0=FS[:, acc, :], in1=FS[:, gf, :], op=ALU.add),
                                       r=[("F", gf), ("F", acc)], w=[("HB", hs)])
                                dst = self.mT_sp[sq][f * 128:(f + 1) * 128, tg * 512:(tg + 1) * 512]
                                sc.add("act", lambda e, dst=dst, hs=hs: e.dma_start(out=dst, in_=HB[:, hs, :]),
                                       r=[("HB", hs)], w=[("mTsp", sq, f, tg)], dma=self.st_hb[hs])

    def resid_matmul(self, L, sq, name, wbt, r0, xfirst):
        sc = self.sc
        BB, FS, WS = self.BB, self.FS, self.WS
        for mp in range(8):
            slot = self.wfill(L, name, wbt, r0, 16, mp * 256, 256)
            for mm in range(2):
                m = mp * 2 + mm
                for tg in range(4):
                    zb = self.zbank()
                    Pz = self.P[zb]
                    xs = self.fslot()
                    src = self.xsrc(xfirst, sq, m, tg)
                    sc.add("sp", lambda e, xs=xs, src=src: e.dma_start(out=FS[:, xs, :], in_=src),
                           r=[self.xkey(xfirst, sq, m, tg)], w=[("F", xs)], dma=self.st_f[xs])
                    for k in range(16):
                        lo = k * 2048 + tg * 512
                        sc.add("pe", lambda e, Pz=Pz, slot=slot, k=k, lo=lo, mm=mm: e.matmul(
                            Pz[:, :], lhsT=WS[:, slot, k, mm * 128:(mm + 1) * 128], rhs=BB[:, lo:lo + 512],
                            start=(k == 0), stop=(k == 15)),
                            r=self.wkeys(slot) + bkeys(lo, lo + 512), w=[("P", zb)])
                    sc.add("dve", lambda e, Pz=Pz, xs=xs: e.tensor_tensor(out=FS[:, xs, :], in0=Pz[:, :], in1=FS[:, xs, :], op=ALU.add),
                           r=[("P", zb), ("F", xs)], w=[("F", xs)])
                    dst = self.yT[sq][m * 128:(m + 1) * 128, tg * 512:(tg + 1) * 512]
                    sc.add("act", lambda e, dst=dst, xs=xs: e.dma_start(out=dst, in_=FS[:, xs, :]),
                           r=[("F", xs)], w=[("yT", sq, m, tg)], dma=self.st_f[xs])

    def wout_phase(self, L, sq, first):
        sc = self.sc
        BB = self.BB
        lops = []
        for c in range(16):
            src = self.mT_sp[sq][c * 128:(c + 1) * 128, :]
            lops.append(sc.add("sp", lambda e, c=c, src=src: e.dma_start(out=BB[:, c * 2048:(c + 1) * 2048], in_=src),
                               r=[("mTsp", sq, c, tg) for tg in range(4)], w=bkeys(c * 2048, (c + 1) * 2048), dma=self.st_bb))
        batch_final(sc, lops)
        self.resid_matmul(L, sq, "out", self.wb_out[L], 0, first)

    def ffn_phase(self, L, sq):
        sc = self.sc
        BB, BH, FS, WS = self.BB, self.BH, self.FS, self.WS
        for qd in range(4):
            for mp in range(8):
                slot = self.wfill(L, "ffi", self.wb_ffi[L], 0, 16, qd * 2048 + mp * 256, 256)
                for mm in range(2):
                    mh = mp * 2 + mm
                    for tg in range(4):
                        zb = self.zbank()
                        Pz = self.P[zb]
                        for k in range(16):
                            sc.add("pe", lambda e, Pz=Pz, slot=slot, k=k, mm=mm, tg=tg: e.matmul(
                                Pz[:, :], lhsT=WS[:, slot, k, mm * 128:(mm + 1) * 128], rhs=BH[:, k, tg * 512:(tg + 1) * 512],
                                start=(k == 0), stop=(k == 15)),
                                r=self.wkeys(slot) + [("H", k, tg)], w=[("P", zb)])
                        rf = self.fslot()
                        sc.add("act", lambda e, Pz=Pz, rf=rf: e.activation(out=FS[:, rf, :], in_=Pz[:, :], func=AF.Relu),
                               r=[("P", zb)], w=[("F", rf)])
                        lo = mh * 2048 + tg * 512
                        sc.add("dve", lambda e, rf=rf, lo=lo: e.tensor_tensor(out=BB[:, lo:lo + 512], in0=FS[:, rf, :], in1=FS[:, rf, :], op=ALU.mult),
                               r=[("F", rf)], w=bkeys(lo, lo + 512))
            self.resid_matmul(L, sq, "ffo", self.wb_ffo[L], qd * 2048, False)

    def emit_all(self):
        nc = self.nc
        sc = self.sc
        from contextlib import ExitStack
        with ExitStack() as es:
            eng_sems = {k: es.enter_context(nc.semaphore("sem_" + k)) for k in ("pe", "act", "dve", "pool", "sp")}
            stream_sems = [es.enter_context(nc.semaphore("st%d" % i)) for i in range(len(sc.streams))]
            block = es.enter_context(nc.Block())

            @block.tensor
            def _(e):
                sc.emit("pe", e, eng_sems, stream_sems)

            @block.scalar
            def _(e):
                sc.emit("act", e, eng_sems, stream_sems)

            @block.vector
            def _(e):
                sc.emit("dve", e, eng_sems, stream_sems)

            @block.gpsimd
            def _(e):
                sc.emit("pool", e, eng_sems, stream_sems)

            @block.sync
            def _(e):
                sc.emit("sp", e, eng_sems, stream_sems)


def _bf(a):
    return np.ascontiguousarray(a.astype(ml_dtypes.bfloat16))


def host_consts():
    k = np.arange(128)[:, None]
    q = np.arange(128)[None, :]
    masks = [(k <= q).astype(np.float32)]
    for o in range(16):
        d = 128 * o + q - k
        m = ((d >= 0) & (d <= 128)).astype(np.float32)
        m += ((d >= 0) & (d % 4 == 0) & (d <= 512)).astype(np.float32)
        m += ((d >= 0) & (d % 16 == 0) & (d <= 2048)).astype(np.float32)
        masks.append(m)
    cmask = _bf(np.concatenate(masks, axis=1))
    sl_a = 2.0 ** (-8.0 * np.arange(1, 5) / 4)
    sl_c = 2.0 ** (-8.0 * np.arange(1, 7) / 6)
    alibi = np.zeros((128, 160), np.float32)
    kl = np.arange(128, dtype=np.float64)
    for h in range(4):
        for o in range(16):
            alibi[:, h * 16 + o] = (-sl_a[h] * (128 * o + 127 - kl)).astype(np.float32)
    for h in range(6):
        for o in range(16):
            alibi[:, 64 + h * 16 + o] = (-np.float64(np.float32(sl_c[h])) * (128 * o + 127 - kl)).astype(np.float32)
    ident = np.eye(128, dtype=np.float32)
    onesd = np.full((128, 128), 1.0 / 2048, np.float32)
    ones128 = np.full((128, 128), 1.0 / 128, np.float32)
    blk = np.zeros((128, 128), np.float32)
    blk[:64, :64] = 1.0 / 64
    blk[64:, 64:] = 1.0 / 64
    cmatb = _bf(np.concatenate([ident, onesd, ones128, blk], axis=1))
    U = (k <= q).astype(np.float32)
    cmatf = np.ascontiguousarray(np.concatenate([U, np.ones((128, 128), np.float32)], axis=1))
    return cmask, alibi, cmatb, cmatf


def host_pvec(inp):
    pv = np.zeros((DEPTH, 128, NP), np.float32)
    for L in range(DEPTH):
        pv[L, :, P_GMIX:P_GMIX + 16] = inp["g_mix_norm"][L].reshape(16, 128).T
        pv[L, :, P_GFFN:P_GFFN + 16] = inp["g_ffn_norm"][L].reshape(16, 128).T
        pv[L, :, P_BGATE:P_BGATE + 48] = inp["b_gate"][L].reshape(48, 128).T
        pv[L, :, P_GQA] = np.tile(inp["g_q_a"][L], 2)
        pv[L, :, P_GKA] = np.tile(inp["g_k_a"][L], 2)
        pv[L, :, P_GQB] = inp["g_q_b"][L]
        pv[L, :, P_GKB] = inp["g_k_b"][L]
        pv[L, :, P_GQC] = inp["g_q_c"][L]
        pv[L, :, P_GKC] = inp["g_k_c"][L]
        pv[L, :, P_LQ1:P_LQ1 + 64] = inp["lam_q1"][L][None, :]
        pv[L, :, P_LK1:P_LK1 + 64] = inp["lam_k1"][L][None, :]
        pv[L, :, P_LQ2:P_LQ2 + 64] = inp["lam_q2"][L][None, :]
        pv[L, :, P_LK2:P_LK2 + 64] = inp["lam_k2"][L][None, :]
        pv[L, :, P_GSUB:P_GSUB + 128] = inp["g_sub_a"][L][None, :]
        pv[L, :, P_BF:P_BF + 64] = np.tile(inp["b_forget"][L], 16)[None, :]
    return pv


_PROG = {}


def get_program(n_layers=DEPTH, n_seq=2, taps=()):
    key = (n_layers, n_seq, tuple(taps))
    if key not in _PROG:
        _PROG[key] = Builder(n_layers, n_seq, taps).build()
    return _PROG[key]


def make_in_maps(inp, n_cores, n_seq):
    x = np.asarray(inp["x"], np.float32)
    cmask, alibi, cmatb, cmatf = host_consts()
    pv = host_pvec({k: np.asarray(v, np.float32) for k, v in inp.items() if k != "x"})
    shared = {
        "w_in": np.ascontiguousarray(inp["w_in"], dtype=np.float32),
        "w_up_a": np.ascontiguousarray(inp["w_up_a"], dtype=np.float32),
        "w_up_b": np.ascontiguousarray(inp["w_up_b"], dtype=np.float32),
        "w_up_c": np.ascontiguousarray(inp["w_up_c"], dtype=np.float32),
        "w_out": np.ascontiguousarray(inp["w_out"], dtype=np.float32),
        "w_ff_in": np.ascontiguousarray(inp["w_ff_in"], dtype=np.float32),
        "w_ff_out": np.ascontiguousarray(inp["w_ff_out"], dtype=np.float32),
        "pvec": pv, "cmask": cmask, "calibi": alibi, "cmatb": cmatb, "cmatf": cmatf,
    }
    maps = []
    for c in range(n_cores):
        xs = x[c * n_seq:(c + 1) * n_seq]
        m = dict(shared)
        m["xT"] = np.ascontiguousarray(np.transpose(xs, (0, 2, 1)))
        maps.append(m)
    return maps


def kernel(**inputs):
    n_cores, n_seq = 8, 2
    nc = get_program(DEPTH, n_seq)
    maps = make_in_maps(inputs, n_cores, n_seq)
    res = run_bass_kernel_spmd(nc, maps, core_ids=list(range(n_cores)))
    outs = []
    for c in range(n_cores):
        yT = np.asarray(res.results[c]["yT"], dtype=np.float32)
        outs.append(np.transpose(yT, (0, 2, 1)))
    return np.ascontiguousarray(np.concatenate(outs, axis=0))
```

```python
import math
import numpy as np
import ml_dtypes
import concourse.bass as bass
import concourse.mybir as mybir
from concourse.bass_utils import run_bass_kernel_spmd

F32 = mybir.dt.float32
BF16 = mybir.dt.bfloat16
AF = mybir.ActivationFunctionType
ALU = mybir.AluOpType
AXX = mybir.AxisListType.X

D = 2048
S = 2048
DEPTH = 4
IN_COLS = 11524
C_QA, C_KA, C_VA, C_QB, C_KB, C_VB, C_FB, C_QC, C_KC, C_VC, C_GL = (
    0, 512, 1024, 1536, 2048, 2560, 3072, 3076, 3844, 4612, 5380)
EPS = 1e-6
NW = 4
NF = 8
NHB = 4
NPT = 4
VST = 132
VOFF = 8 * 2048

P_GMIX, P_GFFN, P_BGATE, P_GQA, P_GKA, P_GQB, P_GKB, P_GQC, P_GKC = 0, 16, 32, 80, 81, 82, 83, 84, 85
P_LQ1, P_LK1, P_LQ2, P_LK2, P_GSUB, P_BF, NP = 86, 150, 214, 278, 342, 470, 534


class Op:
    __slots__ = ("eng", "fn", "deps", "dma", "val", "need", "idx", "stream", "xw")


class Sched:
    ENGS = ("pe", "act", "dve", "pool", "sp")

    def __init__(self):
        self.ops = {k: [] for k in self.ENGS}
        self.lastw = {}
        self.rdr = {}
        self.streams = []

    def new_stream(self):
        self.streams.append(0)
        return len(self.streams) - 1

    def add(self, eng, fn, r=(), w=(), dma=None):
        op = Op()
        op.eng = eng
        op.fn = fn
        op.dma = dma is not None
        op.need = False
        op.xw = None
        op.idx = None
        op.stream = dma
        op.val = None
        if op.dma:
            self.streams[dma] += 1
            op.val = 16 * self.streams[dma]
        deps = []
        seen = set()

        def push(d, raw):
            if d is None or id(d) in seen:
                return
            if (not d.dma) and d.eng == eng and (not op.dma):
                if eng == "pe" or not raw:
                    return
            seen.add(id(d))
            deps.append(d)

        lastw = self.lastw
        rdr = self.rdr
        for k in r:
            push(lastw.get(k), True)
        for k in w:
            push(lastw.get(k), False)
            rd = rdr.get(k)
            if rd:
                for d in rd.values():
                    push(d, False)
        op.deps = deps
        for k in r:
            rd = rdr.get(k)
            if rd is None:
                rd = rdr[k] = {}
            rd[id(op) if op.dma else eng] = op
        for k in w:
            lastw[k] = op
            rdr[k] = {}
        self.ops[eng].append(op)
        return op

    def finalize(self):
        for ops in self.ops.values():
            for op in ops:
                for d in op.deps:
                    d.need = True
        for ops in self.ops.values():
            cnt = 0
            for op in ops:
                if op.need and not op.dma:
                    cnt += 1
                    op.idx = cnt

    def emit(self, eng, e, eng_sems, stream_sems):
        seen = {}
        for op in self.ops[eng]:
            if op.xw:
                for (st, v) in op.xw:
                    e.wait_ge(stream_sems[st], v)
            for d in op.deps:
                if d.dma:
                    key = ("s", d.stream)
                    val = d.val
                    sem = stream_sems[d.stream]
                else:
                    key = ("e", d.eng)
                    val = d.idx
                    sem = eng_sems[d.eng]
                if seen.get(key, 0) >= val:
                    continue
                seen[key] = val
                e.wait_ge(sem, val)
            if op.fn is None:
                continue
            inst = op.fn(e)
            if op.dma:
                inst.then_inc(stream_sems[op.stream], 16)
            elif op.need:
                inst.then_inc(eng_sems[eng], 1)


def batch_final(sc, ops):
    fin = 16 * sc.streams[ops[0].stream]
    for o in ops:
        o.val = fin


def bkeys(lo, hi):
    return [("B", g) for g in range(lo // 512, (hi - 1) // 512 + 1)]


class Builder:
    def __init__(self, n_layers=DEPTH, n_seq=2, taps=(), stop=99):
        self.stop = stop
        self.overlap_rms = False
        self.tap_group = 0
        self.n_layers = n_layers
        self.n_seq = n_seq
        self.taps = set(taps)
        self.tap_list = []
        self.nc = bass.Bass("TRN2", target_bir_lowering=False)
        self.sc = Sched()

    def dram(self):
        nc = self.nc
        NL, NS = self.n_layers, self.n_seq

        def inp(name, shape, dt=F32):
            return nc.dram_tensor(name, shape, dt, kind="ExternalInput").ap()

        def internal(name, shape, dt=BF16):
            return nc.dram_tensor(name, shape, dt, kind="Internal").ap()

        self.xT = inp("xT", [NS, D, S])
        self.yT = nc.dram_tensor("yT", [NS, D, S], F32, kind="ExternalOutput").ap()
        self.w_in = inp("w_in", [DEPTH, D, IN_COLS])
        self.w_up_a = inp("w_up_a", [DEPTH, 512, D])
        self.w_up_b = inp("w_up_b", [DEPTH, 512, D])
        self.w_up_c = inp("w_up_c", [DEPTH, 768, D])
        self.w_out = inp("w_out", [DEPTH, D, D])
        self.w_ffi = inp("w_ff_in", [DEPTH, D, 4 * D])
        self.w_ffo = inp("w_ff_out", [DEPTH, 4 * D, D])
        self.pvec = inp("pvec", [DEPTH, 128, NP])
        self.cmask = inp("cmask", [128, 17 * 128], BF16)
        self.calibi = inp("calibi", [128, 160])
        self.cmatb = inp("cmatb", [128, 4 * 128], BF16)
        self.cmatf = inp("cmatf", [128, 2 * 128])
        self.wb_in = internal("wb_in", [NL, D, IN_COLS])
        self.wb_up_a = internal("wb_up_a", [NL, 512, D])
        self.wb_up_b = internal("wb_up_b", [NL, 512, D])
        self.wb_up_c = internal("wb_up_c", [NL, 768, D])
        self.wb_out = internal("wb_out", [NL, D, D])
        self.wb_ffi = internal("wb_ffi", [NL, D, 4 * D])
        self.wb_ffo = internal("wb_ffo", [NL, 4 * D, D])
        self.oT_sp = internal("oT_sp", [NS, 14 * 128, S])
        self.mT_sp = internal("mT_sp", [NS, D, S])
        self.tap_out = {}
        for name, shape, dt in self.tap_specs():
            if name in self.taps:
                self.tap_out[name] = nc.dram_tensor("tap_" + name, shape, dt, kind="ExternalOutput").ap()

    def tap_specs(self):
        return [("hT", [128, 16 * 2048], BF16), ("BB", [128, 32768], BF16), ("biasB", [128, 1024], F32),
                ("lg", [128, 6 * 64], F32), ("oT", [14 * 128, 2048], BF16), ("mT", [2048, 2048], BF16)]

    def build(self):
        nc = self.nc
        self.dram()
        from contextlib import ExitStack
        with ExitStack() as es:
            def sb(name, shape, dt):
                return es.enter_context(nc.sbuf_tensor(name, shape, dt))

            def ps(name, shape, dt):
                return es.enter_context(nc.psum_tensor(name, shape, dt))

            import os as _os
            if int(_os.environ.get("KSMALL", "0")):
                BH = sb("BH", [128, 16, 256], BF16)
                BB = sb("BB", [128, 4096], BF16)
            else:
                BH = sb("BH", [128, 16, 2048], BF16)
                BB = sb("BB", [128, 32768], BF16)
            WS = sb("WS", [128, NW, 16, 256], BF16)
            FS = sb("FS", [128, NF, 512], F32)
            HB = sb("HB", [128, NHB, 512], BF16)
            PT = sb("PT", [128, NPT, 128], BF16)
            PTW = sb("PTW", [128, 3, 512], BF16)
            self.PTW = PTW
            OTL = sb("OTL", [128, 4, 128], BF16)
            N1 = sb("N1", [128, 4, 128], F32)
            DT = sb("DT", [128, 4, 128], F32)
            JK = sb("JK", [128, 128], F32)
            JK2 = sb("JK2", [128, 128], F32)
            SCL = sb("SCL", [128, 64], F32)
            MASK = sb("MASK", [128, 17 * 128], BF16)
            ALIBI = sb("ALIBI", [128, 160], F32)
            CMB = sb("CMB", [128, 4 * 128], BF16)
            CMF = sb("CMF", [128, 2 * 128], F32)
            PV = sb("PV", [128, NP], F32)
            GSUBK = sb("GSUBK", [128, 128], F32)
            LAMT = sb("LAMT", [128, 8], F32)
            BIASB = sb("BIASB", [128, 4 * 16 * 16], F32)
            LG = sb("LG", [128, 6, 64], F32)
            ONE16 = sb("ONE16", [128, 16], F32)
            P0 = ps("P0", [128, 512], F32)
            P1 = ps("P1", [128, 512], F32)
            P2 = ps("P2", [128, 512], F32)
            P3 = ps("P3", [128, 512], F32)
            P4 = ps("P4", [128, 512], F32)
            P5 = ps("P5", [128, 512], F32)
            T0 = ps("T0", [128, 1024], BF16)
            P6 = ps("P6", [128, 512], F32)
            self.BH, self.BB, self.WS, self.FS, self.HB, self.PT = BH, BB, WS, FS, HB, PT
            self.OTL, self.N1, self.DT, self.JK, self.SCL = OTL, N1, DT, JK, SCL
            self.JK2 = JK2
            self.MASK, self.ALIBI, self.CMB, self.CMF, self.PV = MASK, ALIBI, CMB, CMF, PV
            self.GSUBK, self.LAMT, self.BIASB, self.LG, self.ONE16 = GSUBK, LAMT, BIASB, LG, ONE16
            self.P = [P0, P1, P2, P3, P4, P5, P6]
            self.T = [T0, T0]
            self.init_state()
            self.record()
            self.sc.finalize()
            self.emit_all()
        return nc

    def init_state(self):
        sc = self.sc
        self.st_w = [sc.new_stream() for _ in range(NW)]
        self.st_f = [sc.new_stream() for _ in range(NF)]
        self.st_hb = [sc.new_stream() for _ in range(NHB)]
        self.st_misc = sc.new_stream()
        self.st_bb = sc.new_stream()
        self.st_tap = sc.new_stream()
        self.wi = 0
        self.fi = 0
        self.hi = 0
        self.pti = 0
        self.ptwi = 0
        self.sci = 0
        self.zi = 0
        self.qkn = 0
        self.f_reserved = set()
        self.mzi = 0
        self.conv_ops = {}
        self.prev_conv = None
        self.all_conv = []

    def fslot(self):
        while True:
            i = self.fi
            self.fi = (i + 1) % NF
            if i not in self.f_reserved:
                return i

    def hslot(self):
        i = self.hi
        self.hi = (i + 1) % NHB
        return i

    def ptslot(self):
        i = self.pti
        self.pti = (i + 1) % NPT
        return i

    def ptwslot(self):
        i = self.ptwi
        self.ptwi = (i + 1) % 3
        return i

    def scol(self):
        i = self.sci
        self.sci = (i + 1) % 64
        return i

    def zbank(self):
        i = self.zi
        self.zi = (i + 1) % 2
        return i

    def convert_layer(self, L):
        sc = self.sc

        def conv(name, src, dst, nrows, rows_per):
            st = sc.new_stream()
            ops = []
            KT = 6
            for r0 in range(0, nrows, rows_per):
                s_ap = src[L][r0:r0 + rows_per, :]
                d_ap = dst[L][r0:r0 + rows_per, :]
                o = sc.add("pool", lambda e, s_ap=s_ap, d_ap=d_ap: e.dma_start(out=d_ap, in_=s_ap, max_dma_last_dim=8192),
                           r=[], w=[("wb", L, name, r0)], dma=st)
                xw = []
                if not ops and self.prev_conv is not None:
                    xw.append(self.prev_conv)
                if len(ops) >= KT:
                    xw.append((st, 16 * (len(ops) - KT + 1)))
                o.xw = xw
                ops.append(o)
            fin = 16 * sc.streams[st]
            self.prev_conv = (st, fin)
            self.all_conv.append((st, fin))
            for o in ops:
                o.val = fin
            self.conv_ops[(L, name)] = ops

        import os as _os
        sel = _os.environ.get("KCONV", "")
        if sel:
            tbl = {"in": (self.w_in, self.wb_in, D, 128), "up_a": (self.w_up_a, self.wb_up_a, 512, 512),
                   "up_c": (self.w_up_c, self.wb_up_c, 768, 768), "out": (self.w_out, self.wb_out, D, 512),
                   "ffi": (self.w_ffi, self.wb_ffi, D, 128), "ffo": (self.w_ffo, self.wb_ffo, 4 * D, 512)}
            for nm in sel.split(","):
                conv(nm, *tbl[nm])
            return
        conv("in", self.w_in, self.wb_in, D, 128)
        conv("up_a", self.w_up_a, self.wb_up_a, 512, 512)
        conv("up_b", self.w_up_b, self.wb_up_b, 512, 512)
        conv("up_c", self.w_up_c, self.wb_up_c, 768, 768)
        conv("out", self.w_out, self.wb_out, D, 512)
        conv("ffi", self.w_ffi, self.wb_ffi, D, 128)
        conv("ffo", self.w_ffo, self.wb_ffo, 4 * D, 512)

    def wdeps(self, L, name):
        return self.conv_ops.get((L, name), [])

    def wfill(self, L, name, dram2d, r0, nk, c0, ncols, slot=None, k0=0, part=None):
        sc = self.sc
        if slot is None:
            slot = self.wi
            self.wi = (slot + 1) % NW
        src = dram2d[r0:r0 + nk * 128, c0:c0 + ncols].rearrange("(c p) n -> p c n", p=128)
        dst = self.WS[:, slot, k0:k0 + nk, 0:ncols]
        wk = [("W", slot, p) for p in ((0, 1, 2) if part is None else (part,))]
        op = sc.add("sp", lambda e: e.dma_start(out=dst, in_=src), r=[], w=wk, dma=self.st_w[slot])
        for d in self.wdeps(L, name):
            if all(d is not x for x in op.deps):
                op.deps.append(d)
        self.last_fill_op = op
        return slot

    def wkeys(self, slot):
        return [("W", slot, 0), ("W", slot, 1), ("W", slot, 2)]

    def record(self):
        sc = self.sc
        import os as _os
        if int(_os.environ.get("KNOMISC", "0")):
            for L in range(self.n_layers):
                self.convert_layer(L)
            fo = sc.add("pool", None)
            fo.xw = list(self.all_conv)
            return
        cops = []
        for dst, src, key in ((self.MASK, self.cmask, "MASK"), (self.ALIBI, self.calibi, "ALIBI"),
                              (self.CMB, self.cmatb, "CMB"), (self.CMF, self.cmatf, "CMF")):
            cops.append(sc.add("sp", lambda e, dst=dst, src=src: e.dma_start(out=dst[:, :], in_=src[:, :]), w=[key], dma=self.st_misc))
        batch_final(sc, cops)
        sc.add("dve", lambda e: e.memset(self.ONE16[:, :], 1.0), w=["ONE16"])
        import os as _os
        self.noconv = bool(int(_os.environ.get("KNOCONV", "0")))
        for L in range(self.n_layers):
            if not self.noconv:
                self.convert_layer(L)
        seqs = [(L, sq) for L in range(self.n_layers) for sq in range(self.n_seq)]
        params_done = set()
        rms_done = False
        for idx, (L, sq) in enumerate(seqs):
            if L not in params_done:
                self.layer_params(L)
                params_done.add(L)
            if self.stop <= 0:
                continue
            if not rms_done:
                self.rms_phase(L, sq, first=(L == 0), gbase=P_GMIX)
            rms_done = False
            self.tap("hT", self.BH[:, :, :], [("H", c, t) for c in range(16) for t in range(4)])
            if self.stop <= 1:
                continue
            self.attention_phase(L, sq)
            if self.stop <= 4:
                continue
            self.merge_phase(L, sq)
            if self.stop <= 5:
                continue
            self.wout_phase(L, sq, first=(L == 0))
            if self.stop <= 6:
                continue
            self.rms_phase(L, sq, first=False, gbase=P_GFFN)
            if self.stop <= 7:
                continue
            bg = None
            if idx + 1 < len(seqs) and self.overlap_rms:
                L2, s2 = seqs[idx + 1]
                if L2 not in params_done:
                    self.layer_params(L2)
                    params_done.add(L2)
                bg = self.rms_gen(L2, s2, first=(L2 == 0), gbase=P_GMIX)
                rms_done = True
            self.ffn_phase(L, sq, bg)
        fo = sc.add("pool", None)
        fo.xw = list(self.all_conv)
        keys = [("yT", sq, m, tg) for sq in range(self.n_seq) for m in range(16) for tg in range(4)]
        sc.add("sp", None, r=keys)
        if self.tap_list:
            sc.add("sp", None, r=[("tap", n) for n in self.tap_list])

    def tap(self, name, ap, keys):
        if name not in self.tap_out or name in self.tap_list:
            return
        self.tap_list.append(name)
        dst = self.tap_out[name]
        if len(dst.shape) == 2 and len(ap.shape) == 3:
            dst = dst.rearrange("p (a b) -> p a b", a=ap.shape[1])
        self.sc.add("sp", lambda e: e.dma_start(out=dst, in_=ap), r=keys, w=[("tap", name)], dma=self.st_tap)

    def layer_params(self, L):
        sc = self.sc
        PV = self.PV
        lam_init = 0.8 - 0.6 * math.exp(-0.3 * L)
        sc.add("sp", lambda e: e.dma_start(out=PV[:, :], in_=self.pvec[L]), w=["PV"], dma=self.st_misc)
        LT = self.LAMT
        JK = self.JK
        sc.add("dve", lambda e: e.tensor_tensor(out=JK[:, 0:64], in0=PV[:, P_LQ1:P_LQ1 + 64], in1=PV[:, P_LK1:P_LK1 + 64], op=ALU.mult),
               r=["PV"], w=["JK"])
        sc.add("dve", lambda e: e.reduce_sum(out=LT[:, 0:1], in_=JK[:, 0:64], axis=AXX), r=["JK"], w=[("LT", 0)])
        sc.add("dve", lambda e: e.tensor_tensor(out=JK[:, 64:128], in0=PV[:, P_LQ2:P_LQ2 + 64], in1=PV[:, P_LK2:P_LK2 + 64], op=ALU.mult),
               r=["PV"], w=["JK2"])
        sc.add("dve", lambda e: e.reduce_sum(out=LT[:, 1:2], in_=JK[:, 64:128], axis=AXX), r=["JK2"], w=[("LT", 1)])
        sc.add("act", lambda e: e.activation(out=LT[:, 2:4], in_=LT[:, 0:2], func=AF.Exp), r=[("LT", 0), ("LT", 1)], w=[("LT", 2)])
        sc.add("dve", lambda e: e.tensor_tensor(out=LT[:, 4:5], in0=LT[:, 3:4], in1=LT[:, 2:3], op=ALU.subtract), r=[("LT", 2)], w=[("LT", 4)])
        sc.add("dve", lambda e: e.tensor_scalar(out=LT[:, 5:6], in0=LT[:, 4:5], scalar1=-lam_init, scalar2=None, op0=ALU.add),
               r=[("LT", 4)], w=["NEGLAM"])
        kk = math.sqrt(128.0) * (1.0 - lam_init)
        sc.add("dve", lambda e: e.tensor_scalar(out=self.GSUBK[:, :], in0=PV[:, P_GSUB:P_GSUB + 128], scalar1=kk, scalar2=None, op0=ALU.mult),
               r=["PV"], w=["GSUBK"])

    def xsrc(self, first, sq, c, tg):
        t = self.xT if first else self.yT
        return t[sq][c * 128:(c + 1) * 128, tg * 512:(tg + 1) * 512]

    def xkey(self, first, sq, c, tg):
        return ("xT" if first else "yT", sq, c, tg)

    def rms_phase(self, L, sq, first, gbase):
        for _ in self.rms_gen(L, sq, first, gbase):
            pass

    def rms_gen(self, L, sq, first, gbase):
        sc = self.sc
        FS, HB, BH, PV = self.FS, self.HB, self.BH, self.PV
        ones_d = self.CMB[:, 128:256]
        for tg in range(4):
            ssb = 2 + (tg % 2)
            Pss = self.P[ssb]
            for c in range(16):
                fs = self.fslot()
                src = self.xsrc(first, sq, c, tg)
                sc.add("sp", lambda e, fs=fs, src=src: e.dma_start(out=FS[:, fs, :], in_=src),
                       r=[self.xkey(first, sq, c, tg)], w=[("F", fs)], dma=self.st_f[fs])
                hs = self.hslot()
                sc.add("act", lambda e, fs=fs, hs=hs: e.activation(out=HB[:, hs, :], in_=FS[:, fs, :], func=AF.Square),
                       r=[("F", fs)], w=[("HB", hs)])
                sc.add("pe", lambda e, hs=hs, c=c, Pss=Pss: e.matmul(Pss[:, :], lhsT=ones_d, rhs=HB[:, hs, :], start=(c == 0), stop=(c == 15)),
                       r=[("HB", hs), "CMB"], w=[("P", ssb)])
                yield
            rs = self.fslot()
            self.f_reserved.add(rs)
            sc.add("act", lambda e, rs=rs, Pss=Pss: e.activation(out=FS[:, rs, :], in_=Pss[:, :], func=AF.Ln, bias=EPS),
                   r=[("P", ssb)], w=[("F", rs)])
            sc.add("act", lambda e, rs=rs: e.activation(out=FS[:, rs, :], in_=FS[:, rs, :], func=AF.Exp, scale=-0.5),
                   r=[("F", rs)], w=[("F", rs)])
            for c in range(16):
                fs = self.fslot()
                src = self.xsrc(first, sq, c, tg)
                sc.add("sp", lambda e, fs=fs, src=src: e.dma_start(out=FS[:, fs, :], in_=src),
                       r=[self.xkey(first, sq, c, tg)], w=[("F", fs)], dma=self.st_f[fs])
                sc.add("dve", lambda e, fs=fs, rs=rs, c=c, tg=tg: e.scalar_tensor_tensor(
                    out=BH[:, c, tg * 512:(tg + 1) * 512], in0=FS[:, fs, :], scalar=PV[:, gbase + c:gbase + c + 1],
                    in1=FS[:, rs, :], op0=ALU.mult, op1=ALU.mult),
                    r=[("F", fs), ("F", rs), "PV"], w=[("H", c, tg)])
                yield
            self.f_reserved.discard(rs)

    def attention_phase(self, L, sq):
        groups = [
            ("A", 4, C_QA, C_KA, C_VA, 0),
            ("B", 4, C_QB, C_KB, C_VB, 4),
            ("C", 3, C_QC, C_KC, C_VC, 8),
            ("C", 3, C_QC + 384, C_KC + 384, C_VC + 384, 11),
        ]
        for gi, (kind, nh, qcol, kcol, vcol, oc0) in enumerate(groups):
            self.project_group(L, sq, kind, nh, qcol, kcol, vcol)
            if kind == "B":
                self.forget_bias(L, sq)
            if gi == self.tap_group and L == 0 and sq == 0:
                self.tap("BB", self.BB[:, :], bkeys(0, 32768))
            if self.stop <= 2:
                return
            self.attend_group(L, sq, kind, nh, oc0, c_head0=(3 if gi == 3 else 0))
            if self.stop <= 3 and gi == self.tap_group:
                return

    def project_group(self, L, sq, kind, nh, qcol, kcol, vcol):
        sc = self.sc
        BB, BH, FS, HB, PV, WS = self.BB, self.BH, self.FS, self.HB, self.PV, self.WS
        wb = self.wb_in[L]
        if kind == "A":
            onesm = self.CMB[:, 384:512]
            gq, gk = P_GQA, P_GKA
        elif kind == "B":
            onesm = self.CMB[:, 256:384]
            gq, gk = P_GQB, P_GKB
        else:
            onesm = self.CMB[:, 256:384]
            gq, gk = P_GQC, P_GKC
        pending = []

        def flush():
            while pending:
                pending.pop(0)()

        for (col0, dst0, gcol) in ((qcol, 0, gq), (kcol, 4, gk)):
            ci = 0
            while ci < nh:
                ncn = min(2, nh - ci)
                slot = self.wfill(L, "in", wb, 0, 16, col0 + ci * 128, ncn * 128)
                for mm in range(ncn):
                    ch = dst0 + ci + mm
                    for tg in range(4):
                        zb = (0, 1, 4, 5)[self.qkn % 4]
                        ssb_n = 2 + (self.qkn % 2)
                        self.qkn += 1
                        Pz = self.P[zb]
                        for k in range(16):
                            sc.add("pe", lambda e, Pz=Pz, slot=slot, k=k, mm=mm, tg=tg: e.matmul(
                                Pz[:, :], lhsT=WS[:, slot, k, mm * 128:(mm + 1) * 128], rhs=BH[:, k, tg * 512:(tg + 1) * 512],
                                start=(k == 0), stop=(k == 15)),
                                r=self.wkeys(slot) + [("H", k, tg)], w=[("P", zb)])
                        hs = self.hslot()
                        sc.add("act", lambda e, Pz=Pz, hs=hs: e.activation(out=HB[:, hs, :], in_=Pz[:, :], func=AF.Square),
                               r=[("P", zb)], w=[("HB", hs)])
                        flush()

                        def tail(zb=zb, Pz=Pz, hs=hs, ch=ch, tg=tg, gcol=gcol, ssb=ssb_n):
                            Pss = self.P[ssb]
                            sc.add("pe", lambda e: e.matmul(Pss[:, :], lhsT=onesm, rhs=HB[:, hs, :], start=True, stop=True),
                                   r=[("HB", hs), "CMB"], w=[("P", ssb)])
                            rs = self.fslot()
                            sc.add("act", lambda e: e.activation(out=FS[:, rs, :], in_=Pss[:, :], func=AF.Ln, bias=EPS),
                                   r=[("P", ssb)], w=[("F", rs)])
                            sc.add("act", lambda e: e.activation(out=FS[:, rs, :], in_=FS[:, rs, :], func=AF.Exp, scale=-0.5),
                                   r=[("F", rs)], w=[("F", rs)])
                            lo = ch * 2048 + tg * 512
                            sc.add("dve", lambda e: e.scalar_tensor_tensor(
                                out=BB[:, lo:lo + 512], in0=Pz[:, :], scalar=PV[:, gcol:gcol + 1], in1=FS[:, rs, :],
                                op0=ALU.mult, op1=ALU.mult),
                                r=[("P", zb), ("F", rs), "PV"], w=bkeys(lo, lo + 512))
                        pending.append(tail)
                ci += ncn
        flush()
        vreg = BB[:, VOFF:VOFF + 64 * VST].rearrange("p (n e) -> p n e", e=VST)
        sc.add("dve", lambda e: e.memset(vreg[:, :, 128:129], 1.0), w=bkeys(VOFF, VOFF + 64 * VST))
        ci = 0
        evi = 0
        if nh == 4:
            if self.wi % 2:
                self.wi += 1
            s0 = self.wi % NW
            self.wi = (self.wi + 2) % NW
            self.wfill(L, "in", wb, 0, 16, vcol, 256, slot=s0)
            self.wfill(L, "in", wb, 0, 16, vcol + 256, 256, slot=s0 + 1)
            for tt in range(16):
                zb = self.zbank()
                Pz = self.P[zb]
                for k in range(16):
                    sc.add("pe", lambda e, Pz=Pz, k=k, tt=tt: e.matmul(
                        Pz[:, :], lhsT=BH[:, k, tt * 128:(tt + 1) * 128], rhs=WS[:, s0:s0 + 2, k, :],
                        start=(k == 0), stop=(k == 15)),
                        r=self.wkeys(s0) + self.wkeys(s0 + 1) + [("H", k, tt // 4)], w=[("P", zb)])
                lo = VOFF + (tt * 4) * VST
                dst = BB[:, lo:lo + 4 * VST].rearrange("p (n e) -> p n e", e=VST)[:, :, 0:128]
                srcp = Pz[:, :].rearrange("p (n e) -> p n e", e=128)
                if evi % 2 == 0:
                    sc.add("act", lambda e, dst=dst, srcp=srcp: e.activation(out=dst, in_=srcp, func=AF.Copy),
                           r=[("P", zb)], w=bkeys(lo, lo + 4 * VST))
                else:
                    sc.add("dve", lambda e, dst=dst, srcp=srcp: e.tensor_copy(out=dst, in_=srcp),
                           r=[("P", zb)], w=bkeys(lo, lo + 4 * VST))
                evi += 1
            ci = nh
        while ci < nh:
            ncn = min(2, nh - ci)
            slot = self.wfill(L, "in", wb, 0, 16, vcol + ci * 128, ncn * 128)
            for tt in range(16):
                zb = self.zbank()
                Pz = self.P[zb]
                for k in range(16):
                    sc.add("pe", lambda e, Pz=Pz, slot=slot, k=k, tt=tt, ncn=ncn: e.matmul(
                        Pz[:, 0:ncn * 128], lhsT=BH[:, k, tt * 128:(tt + 1) * 128], rhs=WS[:, slot, k, 0:ncn * 128],
                        start=(k == 0), stop=(k == 15)),
                        r=self.wkeys(slot) + [("H", k, tt // 4)], w=[("P", zb)])
                lo = VOFF + (tt * 4 + ci) * VST
                dst = BB[:, lo:lo + ncn * VST].rearrange("p (n e) -> p n e", e=VST)[:, :, 0:128]
                srcp = Pz[:, 0:ncn * 128].rearrange("p (n e) -> p n e", e=128)
                if evi % 2 == 0:
                    sc.add("act", lambda e, dst=dst, srcp=srcp: e.activation(out=dst, in_=srcp, func=AF.Copy),
                           r=[("P", zb)], w=bkeys(lo, lo + ncn * VST))
                else:
                    sc.add("dve", lambda e, dst=dst, srcp=srcp: e.tensor_copy(out=dst, in_=srcp),
                           r=[("P", zb)], w=bkeys(lo, lo + ncn * VST))
                evi += 1
            ci += ncn

    def forget_bias(self, L, sq):
        sc = self.sc
        BH, WS, PV, LG = self.BH, self.WS, self.PV, self.LG
        wb = self.wb_in[L]
        slot = self.wfill(L, "in", wb, 0, 16, C_FB, 4)
        Pf = self.P[5]
        for tt in range(16):
            for k in range(16):
                sc.add("pe", lambda e, k=k, tt=tt: e.matmul(
                    Pf[:, tt * 4:(tt + 1) * 4], lhsT=BH[:, k, tt * 128:(tt + 1) * 128], rhs=WS[:, slot, k, 0:4],
                    start=(k == 0), stop=(k == 15)),
                    r=self.wkeys(slot) + [("H", k, tt // 4)], w=[("P", 5)])
        sc.add("dve", lambda e: e.tensor_tensor(out=LG[:, 0, :], in0=Pf[:, 0:64], in1=PV[:, P_BF:P_BF + 64], op=ALU.add),
               r=[("P", 5), "PV"], w=[("LG", 0)])
        sc.add("act", lambda e: e.activation(out=LG[:, 1, :], in_=LG[:, 0, :], func=AF.Exp, scale=-1.0), r=[("LG", 0)], w=[("LG", 1)])
        sc.add("act", lambda e: e.activation(out=LG[:, 2, :], in_=LG[:, 1, :], func=AF.Ln, bias=1.0), r=[("LG", 1)], w=[("LG", 2)])
        U = self.CMF[:, 0:128]
        ONES = self.CMF[:, 128:256]
        Pc = self.P[4]
        sc.add("pe", lambda e: e.matmul(Pc[:, 0:64], lhsT=U, rhs=LG[:, 2, :], start=True, stop=True), r=[("LG", 2), "CMF"], w=[("P", 4)])
        sc.add("pe", lambda e: e.matmul(Pc[:, 64:128], lhsT=ONES, rhs=LG[:, 2, :], start=True, stop=True), r=[("LG", 2), "CMF"], w=[("P", 4)])
        sc.add("dve", lambda e: e.tensor_copy(out=LG[:, 3, :], in_=Pc[:, 64:128]), r=[("P", 4)], w=[("LG", 3)])
        for h in range(4):
            sc.add("dve", lambda e, h=h: e.tensor_tensor_scan(out=LG[:, 4, h:64:4], data0=self.ONE16[:, :], data1=LG[:, 3, h:64:4],
                                                               initial=0.0, op0=ALU.mult, op1=ALU.add),
                   r=[("LG", 3), "ONE16"], w=[("LG", 4, h)])
        i4 = [("LG", 4, h) for h in range(4)]
        sc.add("dve", lambda e: e.tensor_tensor(out=LG[:, 5, :], in0=Pc[:, 0:64], in1=LG[:, 4, :], op=ALU.add), r=[("P", 4)] + i4, w=[("LG", 5)])
        sc.add("dve", lambda e: e.tensor_tensor(out=LG[:, 5, :], in0=LG[:, 5, :], in1=LG[:, 3, :], op=ALU.subtract),
               r=[("LG", 5), ("LG", 3)], w=[("LG", 5)])
        BI = self.BIASB
        for h in range(4):
            for i in range(16):
                o0 = (h * 16 + i) * 16
                sc.add("dve", lambda e, h=h, i=i, o0=o0: e.tensor_scalar(
                    out=BI[:, o0:o0 + i + 1], in0=LG[:, 5, h:h + 4 * i + 1:4], scalar1=LG[:, 4, i * 4 + h:i * 4 + h + 1], scalar2=None,
                    op0=ALU.subtract),
                    r=[("LG", 5)] + i4, w=[("BIASB", h, i)])
        self.tap("biasB", BI[:, :], [("BIASB", h, i) for h in range(4) for i in range(16)])
        self.tap("lg", LG[:, :, :], [("LG", 5), ("LG", 3)] + i4)

    def attend_group(self, L, sq, kind, nh, oc0, c_head0=0):
        sc = self.sc
        HB = self.HB
        for h in range(nh):
            units = [(0, 64), (64, 64)] if kind == "A" else [(0, 128)]
            for G in range(4):
                hs_out = self.hslot()
                for ui, (pb, dk) in enumerate(units):
                    self.attend_unit(kind, h, G, ui, pb, dk, c_head0, hs_out)
                oc = oc0 + h
                dst = self.oT_sp[sq][oc * 128:(oc + 1) * 128, G * 512:(G + 1) * 512]
                sc.add("act", lambda e, dst=dst, hs_out=hs_out: e.dma_start(out=dst, in_=HB[:, hs_out, :]),
                       r=[("HB", hs_out)], w=[("oTsp", sq, oc, G)], dma=self.st_hb[hs_out])

    def attend_unit(self, kind, h, G, ui, pb, dk, c_head0, hs_out):
        sc = self.sc
        BB, PT, MASK, ALIBI, BI = self.BB, self.PT, self.MASK, self.ALIBI, self.BIASB
        OTL, N1, DT, SCL, HB = self.OTL, self.N1, self.DT, self.SCL, self.HB
        causal = MASK[:, 0:128]
        scale = 0.125 if kind == "A" else 128.0 ** -0.5
        qrows = BB[pb:pb + dk, h * 2048:(h + 1) * 2048]
        krows = BB[pb:pb + dk, (4 + h) * 2048:(5 + h) * 2048]
        qk = lambda lo, hi: bkeys(h * 2048 + lo, h * 2048 + hi)
        kk = lambda lo, hi: bkeys((4 + h) * 2048 + lo, (4 + h) * 2048 + hi)
        if kind == "A":
            slope = 2.0 ** (-8.0 * (h + 1) / 4)
        elif kind == "C":
            slope = 2.0 ** (-8.0 * (c_head0 + h + 1) / 6)
        else:
            slope = None
        wide = slope is not None and slope * 511.0 <= 40.0
        omax = 15
        if slope is not None and not wide:
            omax = min(15, int(math.ceil(160.0 / (slope * 128.0))) - 1)
        PTW = self.PTW
        j_min = max(0, 4 * G - omax)
        js = list(range(j_min, 4 * G + 4))

        def cols(j):
            i_lo = max(j, 4 * G)
            i_hi = min(4 * G + 3, j + omax)
            return i_lo, i_hi

        SB = (0, 1, 6)

        def s_mm(j):
            i_lo, i_hi = cols(j)
            sb = SB[j % 3]
            Ps = self.P[sb]
            c0 = (i_lo - 4 * G) * 128
            c1 = (i_hi + 1 - 4 * G) * 128
            sc.add("pe", lambda e: e.matmul(Ps[:, c0:c1], lhsT=krows[:, j * 128:(j + 1) * 128],
                                            rhs=qrows[:, i_lo * 128:(i_hi + 1) * 128], start=True, stop=True),
                   r=kk(j * 128, (j + 1) * 128) + qk(i_lo * 128, (i_hi + 1) * 128), w=[("P", sb)])

        def pv(j):
            i_lo, i_hi = cols(j)
            sb = SB[j % 3]
            Ps = self.P[sb]
            vlo = VOFF + (j * 4 + h) * VST
            if wide:
                c0 = (i_lo - 4 * G) * 128
                op_ = 4 * G + 3 - j
                if kind == "A":
                    bias = ALIBI[:, h * 16 + op_:h * 16 + op_ + 1]
                else:
                    hh = c_head0 + h
                    bias = ALIBI[:, 64 + hh * 16 + op_:64 + hh * 16 + op_ + 1]
                ws = self.ptwslot()
                sc.add("act", lambda e: e.activation(out=PTW[:, ws, c0:512], in_=Ps[:, c0:512], func=AF.Exp, bias=bias, scale=scale),
                       r=[("P", sb), "ALIBI"], w=[("PTW", ws)])
                if kind == "A":
                    if j >= 4 * G:
                        sc.add("dve", lambda e: e.tensor_tensor(out=PTW[:, ws, c0:c0 + 128], in0=PTW[:, ws, c0:c0 + 128], in1=causal, op=ALU.mult),
                               r=[("PTW", ws), "MASK"], w=[("PTW", ws)])
                else:
                    o0 = i_lo - j
                    mk = MASK[:, (1 + o0) * 128:(1 + o0) * 128 + (512 - c0)]
                    sc.add("dve", lambda e: e.tensor_tensor(out=PTW[:, ws, c0:512], in0=PTW[:, ws, c0:512], in1=mk, op=ALU.mult),
                           r=[("PTW", ws), "MASK"], w=[("PTW", ws)])
                for i in range(i_lo, i_hi + 1):
                    qi = i - 4 * G
                    Po = self.P[2 + qi]
                    sc.add("pe", lambda e, qi=qi, Po=Po, i=i: e.matmul(
                        Po[:, 0:129], lhsT=PTW[:, ws, qi * 128:(qi + 1) * 128], rhs=BB[:, vlo:vlo + 129],
                        start=(j == max(0, i - omax)), stop=(j == i)),
                        r=[("PTW", ws)] + bkeys(vlo, vlo + 129), w=[("P", 2 + qi)])
                return
            for i in range(i_lo, i_hi + 1):
                qi = i - 4 * G
                o = i - j
                if kind == "A":
                    bias = ALIBI[:, h * 16 + o:h * 16 + o + 1]
                    bkey = ["ALIBI"]
                    mask = causal if o == 0 else None
                elif kind == "B":
                    off = (h * 16 + i) * 16 + j
                    bias = BI[:, off:off + 1]
                    bkey = [("BIASB", h, i)]
                    mask = causal if o == 0 else None
                else:
                    hh = c_head0 + h
                    bias = ALIBI[:, 64 + hh * 16 + o:64 + hh * 16 + o + 1]
                    bkey = ["ALIBI"]
                    mask = MASK[:, (1 + o) * 128:(2 + o) * 128]
                ps = self.ptslot()
                sc.add("act", lambda e, ps=ps, qi=qi, bias=bias: e.activation(
                    out=PT[:, ps, :], in_=Ps[:, qi * 128:(qi + 1) * 128], func=AF.Exp, bias=bias, scale=scale),
                    r=[("P", sb)] + bkey, w=[("PT", ps)])
                if mask is not None:
                    sc.add("dve", lambda e, ps=ps, mask=mask: e.tensor_tensor(out=PT[:, ps, :], in0=PT[:, ps, :], in1=mask, op=ALU.mult),
                           r=[("PT", ps), "MASK"], w=[("PT", ps)])
                Po = self.P[2 + qi]
                sc.add("pe", lambda e, ps=ps, Po=Po, i=i: e.matmul(
                    Po[:, 0:129], lhsT=PT[:, ps, :], rhs=BB[:, vlo:vlo + 129], start=(j == max(0, i - omax)), stop=(j == i)),
                    r=[("PT", ps)] + bkeys(vlo, vlo + 129), w=[("P", 2 + qi)])

        s_mm(js[0])
        if len(js) > 1:
            s_mm(js[1])
        for n, j in enumerate(js):
            if n + 2 < len(js):
                s_mm(js[n + 2])
            pv(j)
        steps = [[] for _ in range(4)]
        for qi in range(4):
            Po = self.P[2 + qi]
            pk = ("P", 2 + qi)
            st = steps[qi]
            if kind == "A" and ui == 0:
                c1 = self.scol()
                st.append(lambda Po=Po, c1=c1, pk=pk: sc.add("dve", lambda e: e.reciprocal(out=SCL[:, c1:c1 + 1], in_=Po[:, 128:129]), r=[pk], w=[("SC", c1)]))
                st.append(lambda Po=Po, c1=c1, qi=qi, pk=pk: sc.add("dve", lambda e: e.tensor_scalar(
                    out=N1[:, qi, :], in0=Po[:, 0:128], scalar1=SCL[:, c1:c1 + 1], scalar2=None, op0=ALU.mult),
                    r=[pk, ("SC", c1)], w=[("N1", qi)]))
                continue
            if kind == "A":
                c1, c2, c3, c4, c5 = (self.scol() for _ in range(5))
                st.append(lambda Po=Po, c1=c1, pk=pk: sc.add("dve", lambda e: e.reciprocal(out=SCL[:, c1:c1 + 1], in_=Po[:, 128:129]), r=[pk], w=[("SC", c1)]))
                st.append(lambda c1=c1, c2=c2: sc.add("dve", lambda e: e.tensor_tensor(out=SCL[:, c2:c2 + 1], in0=SCL[:, c1:c1 + 1], in1=self.LAMT[:, 5:6], op=ALU.mult),
                                                      r=[("SC", c1), "NEGLAM"], w=[("SC", c2)]))
                st.append(lambda Po=Po, c2=c2, qi=qi, pk=pk: sc.add("dve", lambda e: e.scalar_tensor_tensor(
                    out=DT[:, qi, :], in0=Po[:, 0:128], scalar=SCL[:, c2:c2 + 1], in1=N1[:, qi, :], op0=ALU.mult, op1=ALU.add),
                    r=[pk, ("SC", c2), ("N1", qi)], w=[("DT", qi)]))
                st.append(lambda qi=qi, c3=c3: sc.add("act", lambda e: e.activation(out=self.JK2[:, :], in_=DT[:, qi, :], func=AF.Square, accum_out=SCL[:, c3:c3 + 1]),
                                                      r=[("DT", qi)], w=[("SC", c3), "JK2"]))
                st.append(lambda c3=c3, c4=c4: sc.add("act", lambda e: e.activation(out=SCL[:, c4:c4 + 1], in_=SCL[:, c3:c3 + 1], func=AF.Ln, bias=128.0 * EPS),
                                                      r=[("SC", c3)], w=[("SC", c4)]))
                st.append(lambda c4=c4, c5=c5: sc.add("act", lambda e: e.activation(out=SCL[:, c5:c5 + 1], in_=SCL[:, c4:c4 + 1], func=AF.Exp, scale=-0.5),
                                                      r=[("SC", c4)], w=[("SC", c5)]))
                st.append(lambda qi=qi, c5=c5: sc.add("dve", lambda e: e.scalar_tensor_tensor(
                    out=OTL[:, qi, :], in0=DT[:, qi, :], scalar=SCL[:, c5:c5 + 1], in1=self.GSUBK[:, :], op0=ALU.mult, op1=ALU.mult),
                    r=[("DT", qi), ("SC", c5), "GSUBK"], w=[("OTL", qi)]))
            else:
                c1 = self.scol()
                st.append(lambda Po=Po, c1=c1, pk=pk: sc.add("dve", lambda e: e.reciprocal(out=SCL[:, c1:c1 + 1], in_=Po[:, 128:129]), r=[pk], w=[("SC", c1)]))
                st.append(lambda Po=Po, c1=c1, qi=qi, pk=pk: sc.add("dve", lambda e: e.tensor_scalar(
                    out=OTL[:, qi, :], in0=Po[:, 0:128], scalar1=SCL[:, c1:c1 + 1], scalar2=None, op0=ALU.mult),
                    r=[pk, ("SC", c1)], w=[("OTL", qi)]))
        for si in range(max(len(x) for x in steps)):
            for qi in range(4):
                if si < len(steps[qi]):
                    steps[qi][si]()
        if kind == "A" and ui == 0:
            return
        Tb = self.T[0]
        for qi in range(4):
            sc.add("pe", lambda e, qi=qi: e.transpose(out=Tb[:, qi * 128:(qi + 1) * 128], in_=OTL[:, qi, :], identity=self.CMB[:, 0:128]),
                   r=[("OTL", qi), "CMB"], w=[("T", 0)])
        sc.add("dve", lambda e: e.tensor_copy(out=HB[:, hs_out, :], in_=Tb[:, 0:512]), r=[("T", 0)], w=[("HB", hs_out)])

    def merge_phase(self, L, sq):
        sc = self.sc
        BB, BH, FS, HB, PV, WS = self.BB, self.BH, self.FS, self.HB, self.PV, self.WS
        lops = []
        for c in range(14):
            src = self.oT_sp[sq][c * 128:(c + 1) * 128, :]
            lops.append(sc.add("sp", lambda e, c=c, src=src: e.dma_start(out=BB[:, c * 2048:(c + 1) * 2048], in_=src),
                               r=[("oTsp", sq, c, G) for G in range(4)], w=bkeys(c * 2048, (c + 1) * 2048), dma=self.st_bb))
        batch_final(sc, lops)
        branches = [(0, 4, self.wb_up_a, "up_a"), (4, 4, self.wb_up_b, "up_b"), (8, 6, self.wb_up_c, "up_c")]
        for fp in range(8):
            gslots = []
            for b in range(3):
                gslots.append(self.wfill(L, "in", self.wb_in[L], 0, 16, C_GL + b * 2048 + fp * 256, 256))
            us = self.wi
            self.wi = (us + 1) % NW
            fops = []
            for b, (k0, nk, wbt, nm) in enumerate(branches):
                self.wfill(L, nm, wbt[L], 0, nk, fp * 256, 256, slot=us, k0=k0, part=b)
                fops.append(self.last_fill_op)
            batch_final(sc, fops)
            for mm in range(2):
                f = fp * 2 + mm
                for tg in range(4):
                    acc = None
                    for b, (k0, nk, wbt, nm) in enumerate(branches):
                        zb = (0, 1, 4, 5)[self.mzi % 4]
                        ub = (2, 3, 6)[self.mzi % 3]
                        self.mzi += 1
                        Pz = self.P[zb]
                        gs = gslots[b]
                        for k in range(16):
                            sc.add("pe", lambda e, Pz=Pz, gs=gs, k=k, mm=mm, tg=tg: e.matmul(
                                Pz[:, :], lhsT=WS[:, gs, k, mm * 128:(mm + 1) * 128], rhs=BH[:, k, tg * 512:(tg + 1) * 512],
                                start=(k == 0), stop=(k == 15)),
                                r=self.wkeys(gs) + [("H", k, tg)], w=[("P", zb)])
                        gf = self.fslot()
                        bcol = P_BGATE + b * 16 + f
                        sc.add("act", lambda e, Pz=Pz, gf=gf, bcol=bcol: e.activation(
                            out=FS[:, gf, :], in_=Pz[:, :], func=AF.Sigmoid, bias=PV[:, bcol:bcol + 1]),
                            r=[("P", zb), "PV"], w=[("F", gf)])
                        Pu = self.P[ub]
                        for kk in range(nk):
                            lo = (k0 + kk) * 2048 + tg * 512
                            sc.add("pe", lambda e, Pu=Pu, kk=kk, lo=lo, k0=k0, nk=nk, us=us, mm=mm: e.matmul(
                                Pu[:, :], lhsT=WS[:, us, k0 + kk, mm * 128:(mm + 1) * 128], rhs=BB[:, lo:lo + 512],
                                start=(kk == 0), stop=(kk == nk - 1)),
                                r=[("W", us, b)] + bkeys(lo, lo + 512), w=[("P", ub)])
                        if b == 0:
                            acc = self.fslot()
                            sc.add("dve", lambda e, Pu=Pu, gf=gf, acc=acc: e.tensor_tensor(out=FS[:, acc, :], in0=Pu[:, :], in1=FS[:, gf, :], op=ALU.mult),
                                   r=[("P", ub), ("F", gf)], w=[("F", acc)])
                        else:
                            sc.add("dve", lambda e, Pu=Pu, gf=gf: e.tensor_tensor(out=FS[:, gf, :], in0=Pu[:, :], in1=FS[:, gf, :], op=ALU.mult),
                                   r=[("P", ub), ("F", gf)], w=[("F", gf)])
                            if b == 1:
                                sc.add("dve", lambda e, gf=gf, acc=acc: e.tensor_tensor(out=FS[:, acc, :], in0=FS[:, acc, :], in1=FS[:, gf, :], op=ALU.add),
                                       r=[("F", gf), ("F", acc)], w=[("F", acc)])
                            else:
                                hs = self.hslot()
                                sc.add("dve", lambda e, gf=gf, acc=acc, hs=hs: e.tensor_tensor(out=HB[:, hs, :], in0=FS[:, acc, :], in1=FS[:, gf, :], op=ALU.add),
                                       r=[("F", gf), ("F", acc)], w=[("HB", hs)])
                                dst = self.mT_sp[sq][f * 128:(f + 1) * 128, tg * 512:(tg + 1) * 512]
                                sc.add("act", lambda e, dst=dst, hs=hs: e.dma_start(out=dst, in_=HB[:, hs, :]),
                                       r=[("HB", hs)], w=[("mTsp", sq, f, tg)], dma=self.st_hb[hs])

    def resid_matmul(self, L, sq, name, wbt, r0, xfirst, bg=None, bg_n=0):
        sc = self.sc
        BB, FS, WS = self.BB, self.FS, self.WS
        for mp in range(8):
            slot = self.wfill(L, name, wbt, r0, 16, mp * 256, 256)
            for mm in range(2):
                m = mp * 2 + mm
                for tg in range(4):
                    zb = self.zbank()
                    Pz = self.P[zb]
                    xs = self.fslot()
                    src = self.xsrc(xfirst, sq, m, tg)
                    sc.add("sp", lambda e, xs=xs, src=src: e.dma_start(out=FS[:, xs, :], in_=src),
                           r=[self.xkey(xfirst, sq, m, tg)], w=[("F", xs)], dma=self.st_f[xs])
                    for k in range(16):
                        lo = k * 2048 + tg * 512
                        sc.add("pe", lambda e, Pz=Pz, slot=slot, k=k, lo=lo, mm=mm: e.matmul(
                            Pz[:, :], lhsT=WS[:, slot, k, mm * 128:(mm + 1) * 128], rhs=BB[:, lo:lo + 512],
                            start=(k == 0), stop=(k == 15)),
                            r=self.wkeys(slot) + bkeys(lo, lo + 512), w=[("P", zb)])
                    sc.add("dve", lambda e, Pz=Pz, xs=xs: e.tensor_tensor(out=FS[:, xs, :], in0=Pz[:, :], in1=FS[:, xs, :], op=ALU.add),
                           r=[("P", zb), ("F", xs)], w=[("F", xs)])
                    dst = self.yT[sq][m * 128:(m + 1) * 128, tg * 512:(tg + 1) * 512]
                    sc.add("act", lambda e, dst=dst, xs=xs: e.dma_start(out=dst, in_=FS[:, xs, :]),
                           r=[("F", xs)], w=[("yT", sq, m, tg)], dma=self.st_f[xs])
                    if bg is not None:
                        for _ in range(bg_n):
                            next(bg, None)

    def wout_phase(self, L, sq, first):
        sc = self.sc
        BB = self.BB
        lops = []
        for c in range(16):
            src = self.mT_sp[sq][c * 128:(c + 1) * 128, :]
            lops.append(sc.add("sp", lambda e, c=c, src=src: e.dma_start(out=BB[:, c * 2048:(c + 1) * 2048], in_=src),
                               r=[("mTsp", sq, c, tg) for tg in range(4)], w=bkeys(c * 2048, (c + 1) * 2048), dma=self.st_bb))
        batch_final(sc, lops)
        self.resid_matmul(L, sq, "out", self.wb_out[L], 0, first)

    def ffn_phase(self, L, sq, bg=None):
        sc = self.sc
        BB, BH, FS, WS = self.BB, self.BH, self.FS, self.WS
        for qd in range(4):
            for mp in range(8):
                slot = self.wfill(L, "ffi", self.wb_ffi[L], 0, 16, qd * 2048 + mp * 256, 256)
                for mm in range(2):
                    mh = mp * 2 + mm
                    for tg in range(4):
                        zb = self.zbank()
                        Pz = self.P[zb]
                        for k in range(16):
                            sc.add("pe", lambda e, Pz=Pz, slot=slot, k=k, mm=mm, tg=tg: e.matmul(
                                Pz[:, :], lhsT=WS[:, slot, k, mm * 128:(mm + 1) * 128], rhs=BH[:, k, tg * 512:(tg + 1) * 512],
                                start=(k == 0), stop=(k == 15)),
                                r=self.wkeys(slot) + [("H", k, tg)], w=[("P", zb)])
                        rf = self.fslot()
                        sc.add("act", lambda e, Pz=Pz, rf=rf: e.activation(out=FS[:, rf, :], in_=Pz[:, :], func=AF.Relu),
                               r=[("P", zb)], w=[("F", rf)])
                        lo = mh * 2048 + tg * 512
                        sc.add("dve", lambda e, rf=rf, lo=lo: e.tensor_tensor(out=BB[:, lo:lo + 512], in0=FS[:, rf, :], in1=FS[:, rf, :], op=ALU.mult),
                               r=[("F", rf)], w=bkeys(lo, lo + 512))
            if qd == 3 and bg is not None:
                self.resid_matmul(L, sq, "ffo", self.wb_ffo[L], qd * 2048, False, bg=bg, bg_n=2)
                for _ in bg:
                    pass
            else:
                self.resid_matmul(L, sq, "ffo", self.wb_ffo[L], qd * 2048, False)

    def emit_all(self):
        nc = self.nc
        sc = self.sc
        from contextlib import ExitStack
        with ExitStack() as es:
            eng_sems = {k: es.enter_context(nc.semaphore("sem_" + k)) for k in ("pe", "act", "dve", "pool", "sp")}
            stream_sems = [es.enter_context(nc.semaphore("st%d" % i)) for i in range(len(sc.streams))]
            block = es.enter_context(nc.Block())

            @block.tensor
            def _(e):
                sc.emit("pe", e, eng_sems, stream_sems)

            @block.scalar
            def _(e):
                sc.emit("act", e, eng_sems, stream_sems)

            @block.vector
            def _(e):
                sc.emit("dve", e, eng_sems, stream_sems)

            @block.gpsimd
            def _(e):
                sc.emit("pool", e, eng_sems, stream_sems)

            @block.sync
            def _(e):
                sc.emit("sp", e, eng_sems, stream_sems)


def _bf(a):
    return np.ascontiguousarray(a.astype(ml_dtypes.bfloat16))


def host_consts():
    k = np.arange(128)[:, None]
    q = np.arange(128)[None, :]
    masks = [(k <= q).astype(np.float32)]
    for o in range(16):
        d = 128 * o + q - k
        m = ((d >= 0) & (d <= 128)).astype(np.float32)
        m += ((d >= 0) & (d % 4 == 0) & (d <= 512)).astype(np.float32)
        m += ((d >= 0) & (d % 16 == 0) & (d <= 2048)).astype(np.float32)
        masks.append(m)
    cmask = _bf(np.concatenate(masks, axis=1))
    sl_a = 2.0 ** (-8.0 * np.arange(1, 5) / 4)
    sl_c = 2.0 ** (-8.0 * np.arange(1, 7) / 6)
    alibi = np.zeros((128, 160), np.float32)
    kl = np.arange(128, dtype=np.float64)
    for h in range(4):
        for o in range(16):
            alibi[:, h * 16 + o] = (-sl_a[h] * (128 * o + 127 - kl)).astype(np.float32)
    for h in range(6):
        for o in range(16):
            alibi[:, 64 + h * 16 + o] = (-np.float64(np.float32(sl_c[h])) * (128 * o + 127 - kl)).astype(np.float32)
    ident = np.eye(128, dtype=np.float32)
    onesd = np.full((128, 128), 1.0 / 2048, np.float32)
    ones128 = np.full((128, 128), 1.0 / 128, np.float32)
    blk = np.zeros((128, 128), np.float32)
    blk[:64, :64] = 1.0 / 64
    blk[64:, 64:] = 1.0 / 64
    cmatb = _bf(np.concatenate([ident, onesd, ones128, blk], axis=1))
    U = (k <= q).astype(np.float32)
    cmatf = np.ascontiguousarray(np.concatenate([U, np.ones((128, 128), np.float32)], axis=1))
    return cmask, alibi, cmatb, cmatf


def host_pvec(inp):
    pv = np.zeros((DEPTH, 128, NP), np.float32)
    for L in range(DEPTH):
        pv[L, :, P_GMIX:P_GMIX + 16] = inp["g_mix_norm"][L].reshape(16, 128).T
        pv[L, :, P_GFFN:P_GFFN + 16] = inp["g_ffn_norm"][L].reshape(16, 128).T
        pv[L, :, P_BGATE:P_BGATE + 48] = inp["b_gate"][L].reshape(48, 128).T
        pv[L, :, P_GQA] = np.tile(inp["g_q_a"][L], 2)
        pv[L, :, P_GKA] = np.tile(inp["g_k_a"][L], 2)
        pv[L, :, P_GQB] = inp["g_q_b"][L]
        pv[L, :, P_GKB] = inp["g_k_b"][L]
        pv[L, :, P_GQC] = inp["g_q_c"][L]
        pv[L, :, P_GKC] = inp["g_k_c"][L]
        pv[L, :, P_LQ1:P_LQ1 + 64] = inp["lam_q1"][L][None, :]
        pv[L, :, P_LK1:P_LK1 + 64] = inp["lam_k1"][L][None, :]
        pv[L, :, P_LQ2:P_LQ2 + 64] = inp["lam_q2"][L][None, :]
        pv[L, :, P_LK2:P_LK2 + 64] = inp["lam_k2"][L][None, :]
        pv[L, :, P_GSUB:P_GSUB + 128] = inp["g_sub_a"][L][None, :]
        pv[L, :, P_BF:P_BF + 64] = np.tile(inp["b_forget"][L], 16)[None, :]
    return pv


_PROG = {}


def get_program(n_layers=DEPTH, n_seq=2, taps=()):
    key = (n_layers, n_seq, tuple(taps))
    if key not in _PROG:
        _PROG[key] = Builder(n_layers, n_seq, taps).build()
    return _PROG[key]


def make_in_maps(inp, n_cores, n_seq):
    x = np.asarray(inp["x"], np.float32)
    cmask, alibi, cmatb, cmatf = host_consts()
    pv = host_pvec({k: np.asarray(v, np.float32) for k, v in inp.items() if k != "x"})
    shared = {
        "w_in": np.ascontiguousarray(inp["w_in"], dtype=np.float32),
        "w_up_a": np.ascontiguousarray(inp["w_up_a"], dtype=np.float32),
        "w_up_b": np.ascontiguousarray(inp["w_up_b"], dtype=np.float32),
        "w_up_c": np.ascontiguousarray(inp["w_up_c"], dtype=np.float32),
        "w_out": np.ascontiguousarray(inp["w_out"], dtype=np.float32),
        "w_ff_in": np.ascontiguousarray(inp["w_ff_in"], dtype=np.float32),
        "w_ff_out": np.ascontiguousarray(inp["w_ff_out"], dtype=np.float32),
        "pvec": pv, "cmask": cmask, "calibi": alibi, "cmatb": cmatb, "cmatf": cmatf,
    }
    maps = []
    for c in range(n_cores):
        xs = x[c * n_seq:(c + 1) * n_seq]
        m = dict(shared)
        m["xT"] = np.ascontiguousarray(np.transpose(xs, (0, 2, 1)))
        maps.append(m)
    return maps


def kernel(**inputs):
    n_cores, n_seq = 8, 2
    nc = get_program(DEPTH, n_seq)
    maps = make_in_maps(inputs, n_cores, n_seq)
    res = run_bass_kernel_spmd(nc, maps, core_ids=list(range(n_cores)))
    outs = []
    for c in range(n_cores):
        yT = np.asarray(res.results[c]["yT"], dtype=np.float32)
        outs.append(np.transpose(yT, (0, 2, 1)))
    return np.ascontiguousarray(np.concatenate(outs, axis=0))
```

```python
import math
import numpy as np
import ml_dtypes
import concourse.bass as bass
import concourse.mybir as mybir
from concourse.bass_utils import run_bass_kernel_spmd

F32 = mybir.dt.float32
BF16 = mybir.dt.bfloat16
AF = mybir.ActivationFunctionType
ALU = mybir.AluOpType
AXX = mybir.AxisListType.X

D = 2048
S = 2048
DEPTH = 4
IN_COLS = 11524
C_QA, C_KA, C_VA, C_QB, C_KB, C_VB, C_FB, C_QC, C_KC, C_VC, C_GL = (
    0, 512, 1024, 1536, 2048, 2560, 3072, 3076, 3844, 4612, 5380)
EPS = 1e-6
NW = 4
NF = 8
NHB = 4
NPT = 4
VST = 132
VOFF = 8 * 2048

P_GMIX, P_GFFN, P_BGATE, P_GQA, P_GKA, P_GQB, P_GKB, P_GQC, P_GKC = 0, 16, 32, 80, 81, 82, 83, 84, 85
P_LQ1, P_LK1, P_LQ2, P_LK2, P_GSUB, P_BF, NP = 86, 150, 214, 278, 342, 470, 534


class Op:
    __slots__ = ("eng", "fn", "deps", "dma", "val", "need", "idx", "stream", "xw")


class Sched:
    ENGS = ("pe", "act", "dve", "pool", "sp")

    def __init__(self):
        self.ops = {k: [] for k in self.ENGS}
        self.lastw = {}
        self.rdr = {}
        self.streams = []

    def new_stream(self):
        self.streams.append(0)
        return len(self.streams) - 1

    def add(self, eng, fn, r=(), w=(), dma=None):
        op = Op()
        op.eng = eng
        op.fn = fn
        op.dma = dma is not None
        op.need = False
        op.xw = None
        op.idx = None
        op.stream = dma
        op.val = None
        if op.dma:
            self.streams[dma] += 1
            op.val = 16 * self.streams[dma]
        deps = []
        seen = set()

        def push(d, raw):
            if d is None or id(d) in seen:
                return
            if (not d.dma) and d.eng == eng and (not op.dma):
                if eng == "pe" or not raw:
                    return
            seen.add(id(d))
            deps.append(d)

        lastw = self.lastw
        rdr = self.rdr
        for k in r:
            push(lastw.get(k), True)
        for k in w:
            push(lastw.get(k), False)
            rd = rdr.get(k)
            if rd:
                for d in rd.values():
                    push(d, False)
        op.deps = deps
        for k in r:
            rd = rdr.get(k)
            if rd is None:
                rd = rdr[k] = {}
            rd[id(op) if op.dma else eng] = op
        for k in w:
            lastw[k] = op
            rdr[k] = {}
        self.ops[eng].append(op)
        return op

    def finalize(self):
        for ops in self.ops.values():
            for op in ops:
                for d in op.deps:
                    d.need = True
        for ops in self.ops.values():
            cnt = 0
            for op in ops:
                if op.need and not op.dma:
                    cnt += 1
                    op.idx = cnt

    def emit(self, eng, e, eng_sems, stream_sems):
        seen = {}
        for op in self.ops[eng]:
            if op.xw:
                for (st, v) in op.xw:
                    e.wait_ge(stream_sems[st], v)
            for d in op.deps:
                if d.dma:
                    key = ("s", d.stream)
                    val = d.val
                    sem = stream_sems[d.stream]
                else:
                    key = ("e", d.eng)
                    val = d.idx
                    sem = eng_sems[d.eng]
                if seen.get(key, 0) >= val:
                    continue
                seen[key] = val
                e.wait_ge(sem, val)
            if op.fn is None:
                continue
            inst = op.fn(e)
            if op.dma:
                inst.then_inc(stream_sems[op.stream], 16)
            elif op.need:
                inst.then_inc(eng_sems[eng], 1)


def batch_final(sc, ops):
    fin = 16 * sc.streams[ops[0].stream]
    for o in ops:
        o.val = fin


def bkeys(lo, hi):
    return [("B", g) for g in range(lo // 512, (hi - 1) // 512 + 1)]


class Builder:
    def __init__(self, n_layers=DEPTH, n_seq=2, taps=(), stop=99):
        self.stop = stop
        self.overlap_rms = False
        self.tap_group = 0
        self.n_layers = n_layers
        self.n_seq = n_seq
        self.taps = set(taps)
        self.tap_list = []
        self.nc = bass.Bass("TRN2", target_bir_lowering=False)
        self.sc = Sched()

    def dram(self):
        nc = self.nc
        NL, NS = self.n_layers, self.n_seq

        def inp(name, shape, dt=F32):
            return nc.dram_tensor(name, shape, dt, kind="ExternalInput").ap()

        def internal(name, shape, dt=BF16):
            return nc.dram_tensor(name, shape, dt, kind="Internal").ap()

        self.xT = inp("xT", [NS, D, S])
        self.yT = nc.dram_tensor("yT", [NS, D, S], F32, kind="ExternalOutput").ap()
        self.w_in = inp("w_in", [DEPTH, D, IN_COLS])
        self.w_up_a = inp("w_up_a", [DEPTH, 512, D])
        self.w_up_b = inp("w_up_b", [DEPTH, 512, D])
        self.w_up_c = inp("w_up_c", [DEPTH, 768, D])
        self.w_out = inp("w_out", [DEPTH, D, D])
        self.w_ffi = inp("w_ff_in", [DEPTH, D, 4 * D])
        self.w_ffo = inp("w_ff_out", [DEPTH, 4 * D, D])
        self.pvec = inp("pvec", [DEPTH, 128, NP])
        self.cmask = inp("cmask", [128, 17 * 128], BF16)
        self.calibi = inp("calibi", [128, 160])
        self.cmatb = inp("cmatb", [128, 4 * 128], BF16)
        self.cmatf = inp("cmatf", [128, 2 * 128])
        self.wb_in = internal("wb_in", [NL, D, IN_COLS])
        self.wb_up_a = internal("wb_up_a", [NL, 512, D])
        self.wb_up_b = internal("wb_up_b", [NL, 512, D])
        self.wb_up_c = internal("wb_up_c", [NL, 768, D])
        self.wb_out = internal("wb_out", [NL, D, D])
        self.wb_ffi = internal("wb_ffi", [NL, D, 4 * D])
        self.wb_ffo = internal("wb_ffo", [NL, 4 * D, D])
        self.oT_sp = internal("oT_sp", [NS, 14 * 128, S])
        self.mT_sp = internal("mT_sp", [NS, D, S])
        self.tap_out = {}
        for name, shape, dt in self.tap_specs():
            if name in self.taps:
                self.tap_out[name] = nc.dram_tensor("tap_" + name, shape, dt, kind="ExternalOutput").ap()

    def tap_specs(self):
        return [("hT", [128, 16 * 2048], BF16), ("BB", [128, 32768], BF16), ("biasB", [128, 1024], F32),
                ("lg", [128, 6 * 64], F32), ("oT", [14 * 128, 2048], BF16), ("mT", [2048, 2048], BF16)]

    def build(self):
        nc = self.nc
        self.dram()
        from contextlib import ExitStack
        with ExitStack() as es:
            def sb(name, shape, dt):
                return es.enter_context(nc.sbuf_tensor(name, shape, dt))

            def ps(name, shape, dt):
                return es.enter_context(nc.psum_tensor(name, shape, dt))

            import os as _os
            if int(_os.environ.get("KSMALL", "0")):
                BH = sb("BH", [128, 16, 256], BF16)
                BB = sb("BB", [128, 4096], BF16)
            else:
                BH = sb("BH", [128, 16, 2048], BF16)
                BB = sb("BB", [128, 32768], BF16)
            WS = sb("WS", [128, NW, 16, 256], BF16)
            FS = sb("FS", [128, NF, 512], F32)
            HB = sb("HB", [128, NHB, 512], BF16)
            PT = sb("PT", [128, NPT, 128], BF16)
            PTW = sb("PTW", [128, 3, 512], BF16)
            self.PTW = PTW
            OTL = sb("OTL", [128, 4, 128], BF16)
            N1 = sb("N1", [128, 4, 128], F32)
            DT = sb("DT", [128, 4, 128], F32)
            JK = sb("JK", [128, 128], F32)
            JK2 = sb("JK2", [128, 128], F32)
            SCL = sb("SCL", [128, 64], F32)
            MASK = sb("MASK", [128, 17 * 128], BF16)
            ALIBI = sb("ALIBI", [128, 160], F32)
            CMB = sb("CMB", [128, 4 * 128], BF16)
            CMF = sb("CMF", [128, 2 * 128], F32)
            PV = sb("PV", [128, NP], F32)
            GSUBK = sb("GSUBK", [128, 128], F32)
            LAMT = sb("LAMT", [128, 8], F32)
            BIASB = sb("BIASB", [128, 4 * 16 * 16], F32)
            LG = sb("LG", [128, 6, 64], F32)
            ONE16 = sb("ONE16", [128, 16], F32)
            P0 = ps("P0", [128, 512], F32)
            P1 = ps("P1", [128, 512], F32)
            P2 = ps("P2", [128, 512], F32)
            P3 = ps("P3", [128, 512], F32)
            P4 = ps("P4", [128, 512], F32)
            P5 = ps("P5", [128, 512], F32)
            T0 = ps("T0", [128, 1024], BF16)
            P6 = ps("P6", [128, 512], F32)
            self.BH, self.BB, self.WS, self.FS, self.HB, self.PT = BH, BB, WS, FS, HB, PT
            self.OTL, self.N1, self.DT, self.JK, self.SCL = OTL, N1, DT, JK, SCL
            self.JK2 = JK2
            self.MASK, self.ALIBI, self.CMB, self.CMF, self.PV = MASK, ALIBI, CMB, CMF, PV
            self.GSUBK, self.LAMT, self.BIASB, self.LG, self.ONE16 = GSUBK, LAMT, BIASB, LG, ONE16
            self.P = [P0, P1, P2, P3, P4, P5, P6]
            self.T = [T0, T0]
            self.init_state()
            self.record()
            self.sc.finalize()
            self.emit_all()
        return nc

    def init_state(self):
        sc = self.sc
        self.st_w = [sc.new_stream() for _ in range(NW)]
        self.st_f = [sc.new_stream() for _ in range(NF)]
        self.st_hb = [sc.new_stream() for _ in range(NHB)]
        self.st_misc = sc.new_stream()
        self.st_bb = sc.new_stream()
        self.st_bb4 = [sc.new_stream() for _ in range(4)]
        self.st_tap = sc.new_stream()
        self.wi = 0
        self.fi = 0
        self.hi = 0
        self.pti = 0
        self.ptwi = 0
        self.sci = 0
        self.zi = 0
        self.qkn = 0
        self.f_reserved = set()
        self.mzi = 0
        self.conv_ops = {}
        self.prev_conv = None
        self.all_conv = []

    def fslot(self):
        while True:
            i = self.fi
            self.fi = (i + 1) % NF
            if i not in self.f_reserved:
                return i

    def hslot(self):
        i = self.hi
        self.hi = (i + 1) % NHB
        return i

    def ptslot(self):
        i = self.pti
        self.pti = (i + 1) % NPT
        return i

    def ptwslot(self):
        i = self.ptwi
        self.ptwi = (i + 1) % 3
        return i

    def scol(self):
        i = self.sci
        self.sci = (i + 1) % 64
        return i

    def zbank(self):
        i = self.zi
        self.zi = (i + 1) % 2
        return i

    def convert_layer(self, L):
        sc = self.sc

        def conv(name, src, dst, nrows, rows_per):
            st = sc.new_stream()
            ops = []
            KT = 6
            for r0 in range(0, nrows, rows_per):
                s_ap = src[L][r0:r0 + rows_per, :]
                d_ap = dst[L][r0:r0 + rows_per, :]
                o = sc.add("pool", lambda e, s_ap=s_ap, d_ap=d_ap: e.dma_start(out=d_ap, in_=s_ap, max_dma_last_dim=8192),
                           r=[], w=[("wb", L, name, r0)], dma=st)
                xw = []
                if not ops and self.prev_conv is not None:
                    xw.append(self.prev_conv)
                if len(ops) >= KT:
                    xw.append((st, 16 * (len(ops) - KT + 1)))
                o.xw = xw
                ops.append(o)
            fin = 16 * sc.streams[st]
            self.prev_conv = (st, fin)
            self.all_conv.append((st, fin))
            for o in ops:
                o.val = fin
            self.conv_ops[(L, name)] = ops

        import os as _os
        sel = _os.environ.get("KCONV", "")
        if sel:
            tbl = {"in": (self.w_in, self.wb_in, D, 128), "up_a": (self.w_up_a, self.wb_up_a, 512, 512),
                   "up_c": (self.w_up_c, self.wb_up_c, 768, 768), "out": (self.w_out, self.wb_out, D, 512),
                   "ffi": (self.w_ffi, self.wb_ffi, D, 128), "ffo": (self.w_ffo, self.wb_ffo, 4 * D, 512)}
            for nm in sel.split(","):
                conv(nm, *tbl[nm])
            return
        conv("in", self.w_in, self.wb_in, D, 128)
        conv("up_a", self.w_up_a, self.wb_up_a, 512, 512)
        conv("up_b", self.w_up_b, self.wb_up_b, 512, 512)
        conv("up_c", self.w_up_c, self.wb_up_c, 768, 768)
        conv("out", self.w_out, self.wb_out, D, 512)
        conv("ffi", self.w_ffi, self.wb_ffi, D, 128)
        conv("ffo", self.w_ffo, self.wb_ffo, 4 * D, 512)

    def wdeps(self, L, name):
        return self.conv_ops.get((L, name), [])

    def wfill(self, L, name, dram2d, r0, nk, c0, ncols, slot=None, k0=0, part=None):
        sc = self.sc
        if slot is None:
            slot = self.wi
            self.wi = (slot + 1) % NW
        src = dram2d[r0:r0 + nk * 128, c0:c0 + ncols].rearrange("(c p) n -> p c n", p=128)
        dst = self.WS[:, slot, k0:k0 + nk, 0:ncols]
        wk = [("W", slot, p) for p in ((0, 1, 2) if part is None else (part,))]
        op = sc.add("sp", lambda e: e.dma_start(out=dst, in_=src), r=[], w=wk, dma=self.st_w[slot])
        for d in self.wdeps(L, name):
            if all(d is not x for x in op.deps):
                op.deps.append(d)
        self.last_fill_op = op
        return slot

    def wkeys(self, slot):
        return [("W", slot, 0), ("W", slot, 1), ("W", slot, 2)]

    def record(self):
        sc = self.sc
        import os as _os
        if int(_os.environ.get("KNOMISC", "0")):
            for L in range(self.n_layers):
                self.convert_layer(L)
            fo = sc.add("pool", None)
            fo.xw = list(self.all_conv)
            return
        cops = []
        for dst, src, key in ((self.MASK, self.cmask, "MASK"), (self.ALIBI, self.calibi, "ALIBI"),
                              (self.CMB, self.cmatb, "CMB"), (self.CMF, self.cmatf, "CMF")):
            cops.append(sc.add("sp", lambda e, dst=dst, src=src: e.dma_start(out=dst[:, :], in_=src[:, :]), w=[key], dma=self.st_misc))
        batch_final(sc, cops)
        sc.add("dve", lambda e: e.memset(self.ONE16[:, :], 1.0), w=["ONE16"])
        import os as _os
        self.noconv = bool(int(_os.environ.get("KNOCONV", "0")))
        for L in range(self.n_layers):
            if not self.noconv:
                self.convert_layer(L)
        seqs = [(L, sq) for L in range(self.n_layers) for sq in range(self.n_seq)]
        params_done = set()
        rms_done = False
        for idx, (L, sq) in enumerate(seqs):
            if L not in params_done:
                self.layer_params(L)
                params_done.add(L)
            if self.stop <= 0:
                continue
            if not rms_done:
                self.rms_phase(L, sq, first=(L == 0), gbase=P_GMIX)
            rms_done = False
            self.tap("hT", self.BH[:, :, :], [("H", c, t) for c in range(16) for t in range(4)])
            if self.stop <= 1:
                continue
            self.attention_phase(L, sq)
            if self.stop <= 4:
                continue
            self.merge_phase(L, sq)
            if self.stop <= 5:
                continue
            self.wout_phase(L, sq, first=(L == 0))
            if self.stop <= 6:
                continue
            self.rms_phase(L, sq, first=False, gbase=P_GFFN)
            if self.stop <= 7:
                continue
            bg = None
            if idx + 1 < len(seqs) and self.overlap_rms:
                L2, s2 = seqs[idx + 1]
                if L2 not in params_done:
                    self.layer_params(L2)
                    params_done.add(L2)
                bg = self.rms_gen(L2, s2, first=(L2 == 0), gbase=P_GMIX)
                rms_done = True
            self.ffn_phase(L, sq, bg)
        fo = sc.add("pool", None)
        fo.xw = list(self.all_conv)
        keys = [("yT", sq, m, tg) for sq in range(self.n_seq) for m in range(16) for tg in range(4)]
        sc.add("sp", None, r=keys)
        if self.tap_list:
            sc.add("sp", None, r=[("tap", n) for n in self.tap_list])

    def tap(self, name, ap, keys):
        if name not in self.tap_out or name in self.tap_list:
            return
        self.tap_list.append(name)
        dst = self.tap_out[name]
        if len(dst.shape) == 2 and len(ap.shape) == 3:
            dst = dst.rearrange("p (a b) -> p a b", a=ap.shape[1])
        self.sc.add("sp", lambda e: e.dma_start(out=dst, in_=ap), r=keys, w=[("tap", name)], dma=self.st_tap)

    def layer_params(self, L):
        sc = self.sc
        PV = self.PV
        lam_init = 0.8 - 0.6 * math.exp(-0.3 * L)
        sc.add("sp", lambda e: e.dma_start(out=PV[:, :], in_=self.pvec[L]), w=["PV"], dma=self.st_misc)
        LT = self.LAMT
        JK = self.JK
        sc.add("dve", lambda e: e.tensor_tensor(out=JK[:, 0:64], in0=PV[:, P_LQ1:P_LQ1 + 64], in1=PV[:, P_LK1:P_LK1 + 64], op=ALU.mult),
               r=["PV"], w=["JK"])
        sc.add("dve", lambda e: e.reduce_sum(out=LT[:, 0:1], in_=JK[:, 0:64], axis=AXX), r=["JK"], w=[("LT", 0)])
        sc.add("dve", lambda e: e.tensor_tensor(out=JK[:, 64:128], in0=PV[:, P_LQ2:P_LQ2 + 64], in1=PV[:, P_LK2:P_LK2 + 64], op=ALU.mult),
               r=["PV"], w=["JK2"])
        sc.add("dve", lambda e: e.reduce_sum(out=LT[:, 1:2], in_=JK[:, 64:128], axis=AXX), r=["JK2"], w=[("LT", 1)])
        sc.add("act", lambda e: e.activation(out=LT[:, 2:4], in_=LT[:, 0:2], func=AF.Exp), r=[("LT", 0), ("LT", 1)], w=[("LT", 2)])
        sc.add("dve", lambda e: e.tensor_tensor(out=LT[:, 4:5], in0=LT[:, 3:4], in1=LT[:, 2:3], op=ALU.subtract), r=[("LT", 2)], w=[("LT", 4)])
        sc.add("dve", lambda e: e.tensor_scalar(out=LT[:, 5:6], in0=LT[:, 4:5], scalar1=-lam_init, scalar2=None, op0=ALU.add),
               r=[("LT", 4)], w=["NEGLAM"])
        kk = math.sqrt(128.0) * (1.0 - lam_init)
        sc.add("dve", lambda e: e.tensor_scalar(out=self.GSUBK[:, :], in0=PV[:, P_GSUB:P_GSUB + 128], scalar1=kk, scalar2=None, op0=ALU.mult),
               r=["PV"], w=["GSUBK"])

    def xsrc(self, first, sq, c, tg):
        t = self.xT if first else self.yT
        return t[sq][c * 128:(c + 1) * 128, tg * 512:(tg + 1) * 512]

    def xkey(self, first, sq, c, tg):
        return ("xT" if first else "yT", sq, c, tg)

    def rms_phase(self, L, sq, first, gbase):
        for _ in self.rms_gen(L, sq, first, gbase):
            pass

    def rms_gen(self, L, sq, first, gbase):
        sc = self.sc
        FS, HB, BH, PV = self.FS, self.HB, self.BH, self.PV
        ones_d = self.CMB[:, 128:256]
        for tg in range(4):
            ssb = 2 + (tg % 2)
            Pss = self.P[ssb]
            for c in range(16):
                fs = self.fslot()
                src = self.xsrc(first, sq, c, tg)
                sc.add("sp", lambda e, fs=fs, src=src: e.dma_start(out=FS[:, fs, :], in_=src),
                       r=[self.xkey(first, sq, c, tg)], w=[("F", fs)], dma=self.st_f[fs])
                hs = self.hslot()
                sc.add("act", lambda e, fs=fs, hs=hs: e.activation(out=HB[:, hs, :], in_=FS[:, fs, :], func=AF.Square),
                       r=[("F", fs)], w=[("HB", hs)])
                sc.add("pe", lambda e, hs=hs, c=c, Pss=Pss: e.matmul(Pss[:, :], lhsT=ones_d, rhs=HB[:, hs, :], start=(c == 0), stop=(c == 15)),
                       r=[("HB", hs), "CMB"], w=[("P", ssb)])
                yield
            rs = self.fslot()
            self.f_reserved.add(rs)
            sc.add("act", lambda e, rs=rs, Pss=Pss: e.activation(out=FS[:, rs, :], in_=Pss[:, :], func=AF.Ln, bias=EPS),
                   r=[("P", ssb)], w=[("F", rs)])
            sc.add("act", lambda e, rs=rs: e.activation(out=FS[:, rs, :], in_=FS[:, rs, :], func=AF.Exp, scale=-0.5),
                   r=[("F", rs)], w=[("F", rs)])
            for c in range(16):
                fs = self.fslot()
                src = self.xsrc(first, sq, c, tg)
                sc.add("sp", lambda e, fs=fs, src=src: e.dma_start(out=FS[:, fs, :], in_=src),
                       r=[self.xkey(first, sq, c, tg)], w=[("F", fs)], dma=self.st_f[fs])
                sc.add("dve", lambda e, fs=fs, rs=rs, c=c, tg=tg: e.scalar_tensor_tensor(
                    out=BH[:, c, tg * 512:(tg + 1) * 512], in0=FS[:, fs, :], scalar=PV[:, gbase + c:gbase + c + 1],
                    in1=FS[:, rs, :], op0=ALU.mult, op1=ALU.mult),
                    r=[("F", fs), ("F", rs), "PV"], w=[("H", c, tg)])
                yield
            self.f_reserved.discard(rs)

    def attention_phase(self, L, sq):
        groups = [
            ("A", 4, C_QA, C_KA, C_VA, 0),
            ("B", 4, C_QB, C_KB, C_VB, 4),
            ("C", 3, C_QC, C_KC, C_VC, 8),
            ("C", 3, C_QC + 384, C_KC + 384, C_VC + 384, 11),
        ]
        for gi, (kind, nh, qcol, kcol, vcol, oc0) in enumerate(groups):
            self.project_group(L, sq, kind, nh, qcol, kcol, vcol)
            if kind == "B":
                self.forget_bias(L, sq)
            if gi == self.tap_group and L == 0 and sq == 0:
                self.tap("BB", self.BB[:, :], bkeys(0, 32768))
            if self.stop <= 2:
                return
            self.attend_group(L, sq, kind, nh, oc0, c_head0=(3 if gi == 3 else 0))
            if self.stop <= 3 and gi == self.tap_group:
                return

    def project_group(self, L, sq, kind, nh, qcol, kcol, vcol):
        sc = self.sc
        BB, BH, FS, HB, PV, WS = self.BB, self.BH, self.FS, self.HB, self.PV, self.WS
        wb = self.wb_in[L]
        if kind == "A":
            onesm = self.CMB[:, 384:512]
            gq, gk = P_GQA, P_GKA
        elif kind == "B":
            onesm = self.CMB[:, 256:384]
            gq, gk = P_GQB, P_GKB
        else:
            onesm = self.CMB[:, 256:384]
            gq, gk = P_GQC, P_GKC
        pending = []

        def flush():
            while pending:
                pending.pop(0)()

        for (col0, dst0, gcol) in ((qcol, 0, gq), (kcol, 4, gk)):
            ci = 0
            while ci < nh:
                ncn = min(2, nh - ci)
                slot = self.wfill(L, "in", wb, 0, 16, col0 + ci * 128, ncn * 128)
                for mm in range(ncn):
                    ch = dst0 + ci + mm
                    for tg in range(4):
                        zb = (0, 1, 4, 5)[self.qkn % 4]
                        ssb_n = 2 + (self.qkn % 2)
                        self.qkn += 1
                        Pz = self.P[zb]
                        for k in range(16):
                            sc.add("pe", lambda e, Pz=Pz, slot=slot, k=k, mm=mm, tg=tg: e.matmul(
                                Pz[:, :], lhsT=WS[:, slot, k, mm * 128:(mm + 1) * 128], rhs=BH[:, k, tg * 512:(tg + 1) * 512],
                                start=(k == 0), stop=(k == 15)),
                                r=self.wkeys(slot) + [("H", k, tg)], w=[("P", zb)])
                        hs = self.hslot()
                        sc.add("act", lambda e, Pz=Pz, hs=hs: e.activation(out=HB[:, hs, :], in_=Pz[:, :], func=AF.Square),
                               r=[("P", zb)], w=[("HB", hs)])
                        flush()

                        def tail(zb=zb, Pz=Pz, hs=hs, ch=ch, tg=tg, gcol=gcol, ssb=ssb_n):
                            Pss = self.P[ssb]
                            sc.add("pe", lambda e: e.matmul(Pss[:, :], lhsT=onesm, rhs=HB[:, hs, :], start=True, stop=True),
                                   r=[("HB", hs), "CMB"], w=[("P", ssb)])
                            rs = self.fslot()
                            sc.add("act", lambda e: e.activation(out=FS[:, rs, :], in_=Pss[:, :], func=AF.Ln, bias=EPS),
                                   r=[("P", ssb)], w=[("F", rs)])
                            sc.add("act", lambda e: e.activation(out=FS[:, rs, :], in_=FS[:, rs, :], func=AF.Exp, scale=-0.5),
                                   r=[("F", rs)], w=[("F", rs)])
                            lo = ch * 2048 + tg * 512
                            sc.add("dve", lambda e: e.scalar_tensor_tensor(
                                out=BB[:, lo:lo + 512], in0=Pz[:, :], scalar=PV[:, gcol:gcol + 1], in1=FS[:, rs, :],
                                op0=ALU.mult, op1=ALU.mult),
                                r=[("P", zb), ("F", rs), "PV"], w=bkeys(lo, lo + 512))
                        pending.append(tail)
                ci += ncn
        flush()
        vreg = BB[:, VOFF:VOFF + 64 * VST].rearrange("p (n e) -> p n e", e=VST)
        sc.add("dve", lambda e: e.memset(vreg[:, :, 128:129], 1.0), w=bkeys(VOFF, VOFF + 64 * VST))
        ci = 0
        evi = 0
        if nh == 4:
            if self.wi % 2:
                self.wi += 1
            s0 = self.wi % NW
            self.wi = (self.wi + 2) % NW
            self.wfill(L, "in", wb, 0, 16, vcol, 256, slot=s0)
            self.wfill(L, "in", wb, 0, 16, vcol + 256, 256, slot=s0 + 1)
            for tt in range(16):
                zb = self.zbank()
                Pz = self.P[zb]
                for k in range(16):
                    sc.add("pe", lambda e, Pz=Pz, k=k, tt=tt: e.matmul(
                        Pz[:, :], lhsT=BH[:, k, tt * 128:(tt + 1) * 128], rhs=WS[:, s0:s0 + 2, k, :],
                        start=(k == 0), stop=(k == 15)),
                        r=self.wkeys(s0) + self.wkeys(s0 + 1) + [("H", k, tt // 4)], w=[("P", zb)])
                lo = VOFF + (tt * 4) * VST
                dst = BB[:, lo:lo + 4 * VST].rearrange("p (n e) -> p n e", e=VST)[:, :, 0:128]
                srcp = Pz[:, :].rearrange("p (n e) -> p n e", e=128)
                if evi % 2 == 0:
                    sc.add("act", lambda e, dst=dst, srcp=srcp: e.activation(out=dst, in_=srcp, func=AF.Copy),
                           r=[("P", zb)], w=bkeys(lo, lo + 4 * VST))
                else:
                    sc.add("dve", lambda e, dst=dst, srcp=srcp: e.tensor_copy(out=dst, in_=srcp),
                           r=[("P", zb)], w=bkeys(lo, lo + 4 * VST))
                evi += 1
            ci = nh
        while ci < nh:
            ncn = min(2, nh - ci)
            slot = self.wfill(L, "in", wb, 0, 16, vcol + ci * 128, ncn * 128)
            for tt in range(16):
                zb = self.zbank()
                Pz = self.P[zb]
                for k in range(16):
                    sc.add("pe", lambda e, Pz=Pz, slot=slot, k=k, tt=tt, ncn=ncn: e.matmul(
                        Pz[:, 0:ncn * 128], lhsT=BH[:, k, tt * 128:(tt + 1) * 128], rhs=WS[:, slot, k, 0:ncn * 128],
                        start=(k == 0), stop=(k == 15)),
                        r=self.wkeys(slot) + [("H", k, tt // 4)], w=[("P", zb)])
                lo = VOFF + (tt * 4 + ci) * VST
                dst = BB[:, lo:lo + ncn * VST].rearrange("p (n e) -> p n e", e=VST)[:, :, 0:128]
                srcp = Pz[:, 0:ncn * 128].rearrange("p (n e) -> p n e", e=128)
                if evi % 2 == 0:
                    sc.add("act", lambda e, dst=dst, srcp=srcp: e.activation(out=dst, in_=srcp, func=AF.Copy),
                           r=[("P", zb)], w=bkeys(lo, lo + ncn * VST))
                else:
                    sc.add("dve", lambda e, dst=dst, srcp=srcp: e.tensor_copy(out=dst, in_=srcp),
                           r=[("P", zb)], w=bkeys(lo, lo + ncn * VST))
                evi += 1
            ci += ncn

    def forget_bias(self, L, sq):
        sc = self.sc
        BH, WS, PV, LG = self.BH, self.WS, self.PV, self.LG
        wb = self.wb_in[L]
        slot = self.wfill(L, "in", wb, 0, 16, C_FB, 4)
        Pf = self.P[5]
        for tt in range(16):
            for k in range(16):
                sc.add("pe", lambda e, k=k, tt=tt: e.matmul(
                    Pf[:, tt * 4:(tt + 1) * 4], lhsT=BH[:, k, tt * 128:(tt + 1) * 128], rhs=WS[:, slot, k, 0:4],
                    start=(k == 0), stop=(k == 15)),
                    r=self.wkeys(slot) + [("H", k, tt // 4)], w=[("P", 5)])
        sc.add("dve", lambda e: e.tensor_tensor(out=LG[:, 0, :], in0=Pf[:, 0:64], in1=PV[:, P_BF:P_BF + 64], op=ALU.add),
               r=[("P", 5), "PV"], w=[("LG", 0)])
        sc.add("act", lambda e: e.activation(out=LG[:, 1, :], in_=LG[:, 0, :], func=AF.Exp, scale=-1.0), r=[("LG", 0)], w=[("LG", 1)])
        sc.add("act", lambda e: e.activation(out=LG[:, 2, :], in_=LG[:, 1, :], func=AF.Ln, bias=1.0), r=[("LG", 1)], w=[("LG", 2)])
        U = self.CMF[:, 0:128]
        ONES = self.CMF[:, 128:256]
        Pc = self.P[4]
        sc.add("pe", lambda e: e.matmul(Pc[:, 0:64], lhsT=U, rhs=LG[:, 2, :], start=True, stop=True), r=[("LG", 2), "CMF"], w=[("P", 4)])
        sc.add("pe", lambda e: e.matmul(Pc[:, 64:128], lhsT=ONES, rhs=LG[:, 2, :], start=True, stop=True), r=[("LG", 2), "CMF"], w=[("P", 4)])
        sc.add("dve", lambda e: e.tensor_copy(out=LG[:, 3, :], in_=Pc[:, 64:128]), r=[("P", 4)], w=[("LG", 3)])
        for h in range(4):
            sc.add("dve", lambda e, h=h: e.tensor_tensor_scan(out=LG[:, 4, h:64:4], data0=self.ONE16[:, :], data1=LG[:, 3, h:64:4],
                                                               initial=0.0, op0=ALU.mult, op1=ALU.add),
                   r=[("LG", 3), "ONE16"], w=[("LG", 4, h)])
        i4 = [("LG", 4, h) for h in range(4)]
        sc.add("dve", lambda e: e.tensor_tensor(out=LG[:, 5, :], in0=Pc[:, 0:64], in1=LG[:, 4, :], op=ALU.add), r=[("P", 4)] + i4, w=[("LG", 5)])
        sc.add("dve", lambda e: e.tensor_tensor(out=LG[:, 5, :], in0=LG[:, 5, :], in1=LG[:, 3, :], op=ALU.subtract),
               r=[("LG", 5), ("LG", 3)], w=[("LG", 5)])
        BI = self.BIASB
        for h in range(4):
            for i in range(16):
                o0 = (h * 16 + i) * 16
                sc.add("dve", lambda e, h=h, i=i, o0=o0: e.tensor_scalar(
                    out=BI[:, o0:o0 + i + 1], in0=LG[:, 5, h:h + 4 * i + 1:4], scalar1=LG[:, 4, i * 4 + h:i * 4 + h + 1], scalar2=None,
                    op0=ALU.subtract),
                    r=[("LG", 5)] + i4, w=[("BIASB", h, i)])
        self.tap("biasB", BI[:, :], [("BIASB", h, i) for h in range(4) for i in range(16)])
        self.tap("lg", LG[:, :, :], [("LG", 5), ("LG", 3)] + i4)

    def attend_group(self, L, sq, kind, nh, oc0, c_head0=0):
        sc = self.sc
        HB = self.HB
        for h in range(nh):
            units = [(0, 64), (64, 64)] if kind == "A" else [(0, 128)]
            for G in range(4):
                hs_out = self.hslot()
                for ui, (pb, dk) in enumerate(units):
                    self.attend_unit(kind, h, G, ui, pb, dk, c_head0, hs_out)
                oc = oc0 + h
                dst = self.oT_sp[sq][oc * 128:(oc + 1) * 128, G * 512:(G + 1) * 512]
                sc.add("act", lambda e, dst=dst, hs_out=hs_out: e.dma_start(out=dst, in_=HB[:, hs_out, :]),
                       r=[("HB", hs_out)], w=[("oTsp", sq, oc, G)], dma=self.st_hb[hs_out])

    def attend_unit(self, kind, h, G, ui, pb, dk, c_head0, hs_out):
        sc = self.sc
        BB, PT, MASK, ALIBI, BI = self.BB, self.PT, self.MASK, self.ALIBI, self.BIASB
        OTL, N1, DT, SCL, HB = self.OTL, self.N1, self.DT, self.SCL, self.HB
        causal = MASK[:, 0:128]
        scale = 0.125 if kind == "A" else 128.0 ** -0.5
        qrows = BB[pb:pb + dk, h * 2048:(h + 1) * 2048]
        krows = BB[pb:pb + dk, (4 + h) * 2048:(5 + h) * 2048]
        qk = lambda lo, hi: bkeys(h * 2048 + lo, h * 2048 + hi)
        kk = lambda lo, hi: bkeys((4 + h) * 2048 + lo, (4 + h) * 2048 + hi)
        if kind == "A":
            slope = 2.0 ** (-8.0 * (h + 1) / 4)
        elif kind == "C":
            slope = 2.0 ** (-8.0 * (c_head0 + h + 1) / 6)
        else:
            slope = None
        wide = slope is not None and slope * 511.0 <= 40.0
        omax = 15
        if slope is not None and not wide:
            omax = min(15, int(math.ceil(160.0 / (slope * 128.0))) - 1)
        PTW = self.PTW
        j_min = max(0, 4 * G - omax)
        js = list(range(j_min, 4 * G + 4))

        def cols(j):
            i_lo = max(j, 4 * G)
            i_hi = min(4 * G + 3, j + omax)
            return i_lo, i_hi

        SB = (0, 1, 6)

        def s_mm(j):
            i_lo, i_hi = cols(j)
            sb = SB[j % 3]
            Ps = self.P[sb]
            c0 = (i_lo - 4 * G) * 128
            c1 = (i_hi + 1 - 4 * G) * 128
            sc.add("pe", lambda e: e.matmul(Ps[:, c0:c1], lhsT=krows[:, j * 128:(j + 1) * 128],
                                            rhs=qrows[:, i_lo * 128:(i_hi + 1) * 128], start=True, stop=True),
                   r=kk(j * 128, (j + 1) * 128) + qk(i_lo * 128, (i_hi + 1) * 128), w=[("P", sb)])

        def pv(j):
            i_lo, i_hi = cols(j)
            sb = SB[j % 3]
            Ps = self.P[sb]
            vlo = VOFF + (j * 4 + h) * VST
            if wide:
                c0 = (i_lo - 4 * G) * 128
                op_ = 4 * G + 3 - j
                if kind == "A":
                    bias = ALIBI[:, h * 16 + op_:h * 16 + op_ + 1]
                else:
                    hh = c_head0 + h
                    bias = ALIBI[:, 64 + hh * 16 + op_:64 + hh * 16 + op_ + 1]
                ws = self.ptwslot()
                sc.add("act", lambda e: e.activation(out=PTW[:, ws, c0:512], in_=Ps[:, c0:512], func=AF.Exp, bias=bias, scale=scale),
                       r=[("P", sb), "ALIBI"], w=[("PTW", ws)])
                if kind == "A":
                    if j >= 4 * G:
                        sc.add("dve", lambda e: e.tensor_tensor(out=PTW[:, ws, c0:c0 + 128], in0=PTW[:, ws, c0:c0 + 128], in1=causal, op=ALU.mult),
                               r=[("PTW", ws), "MASK"], w=[("PTW", ws)])
                else:
                    o0 = i_lo - j
                    mk = MASK[:, (1 + o0) * 128:(1 + o0) * 128 + (512 - c0)]
                    sc.add("dve", lambda e: e.tensor_tensor(out=PTW[:, ws, c0:512], in0=PTW[:, ws, c0:512], in1=mk, op=ALU.mult),
                           r=[("PTW", ws), "MASK"], w=[("PTW", ws)])
                for i in range(i_lo, i_hi + 1):
                    qi = i - 4 * G
                    Po = self.P[2 + qi]
                    sc.add("pe", lambda e, qi=qi, Po=Po, i=i: e.matmul(
                        Po[:, 0:129], lhsT=PTW[:, ws, qi * 128:(qi + 1) * 128], rhs=BB[:, vlo:vlo + 129],
                        start=(j == max(0, i - omax)), stop=(j == i)),
                        r=[("PTW", ws)] + bkeys(vlo, vlo + 129), w=[("P", 2 + qi)])
                return
            for i in range(i_lo, i_hi + 1):
                qi = i - 4 * G
                o = i - j
                if kind == "A":
                    bias = ALIBI[:, h * 16 + o:h * 16 + o + 1]
                    bkey = ["ALIBI"]
                    mask = causal if o == 0 else None
                elif kind == "B":
                    off = (h * 16 + i) * 16 + j
                    bias = BI[:, off:off + 1]
                    bkey = [("BIASB", h, i)]
                    mask = causal if o == 0 else None
                else:
                    hh = c_head0 + h
                    bias = ALIBI[:, 64 + hh * 16 + o:64 + hh * 16 + o + 1]
                    bkey = ["ALIBI"]
                    mask = MASK[:, (1 + o) * 128:(2 + o) * 128]
                ps = self.ptslot()
                sc.add("act", lambda e, ps=ps, qi=qi, bias=bias: e.activation(
                    out=PT[:, ps, :], in_=Ps[:, qi * 128:(qi + 1) * 128], func=AF.Exp, bias=bias, scale=scale),
                    r=[("P", sb)] + bkey, w=[("PT", ps)])
                if mask is not None:
                    sc.add("dve", lambda e, ps=ps, mask=mask: e.tensor_tensor(out=PT[:, ps, :], in0=PT[:, ps, :], in1=mask, op=ALU.mult),
                           r=[("PT", ps), "MASK"], w=[("PT", ps)])
                Po = self.P[2 + qi]
                sc.add("pe", lambda e, ps=ps, Po=Po, i=i: e.matmul(
                    Po[:, 0:129], lhsT=PT[:, ps, :], rhs=BB[:, vlo:vlo + 129], start=(j == max(0, i - omax)), stop=(j == i)),
                    r=[("PT", ps)] + bkeys(vlo, vlo + 129), w=[("P", 2 + qi)])

        s_mm(js[0])
        if len(js) > 1:
            s_mm(js[1])
        for n, j in enumerate(js):
            if n + 2 < len(js):
                s_mm(js[n + 2])
            pv(j)
        steps = [[] for _ in range(4)]
        for qi in range(4):
            Po = self.P[2 + qi]
            pk = ("P", 2 + qi)
            st = steps[qi]
            if kind == "A" and ui == 0:
                c1 = self.scol()
                st.append(lambda Po=Po, c1=c1, pk=pk: sc.add("dve", lambda e: e.reciprocal(out=SCL[:, c1:c1 + 1], in_=Po[:, 128:129]), r=[pk], w=[("SC", c1)]))
                st.append(lambda Po=Po, c1=c1, qi=qi, pk=pk: sc.add("dve", lambda e: e.tensor_scalar(
                    out=N1[:, qi, :], in0=Po[:, 0:128], scalar1=SCL[:, c1:c1 + 1], scalar2=None, op0=ALU.mult),
                    r=[pk, ("SC", c1)], w=[("N1", qi)]))
                continue
            if kind == "A":
                c1, c2, c3, c4, c5 = (self.scol() for _ in range(5))
                st.append(lambda Po=Po, c1=c1, pk=pk: sc.add("dve", lambda e: e.reciprocal(out=SCL[:, c1:c1 + 1], in_=Po[:, 128:129]), r=[pk], w=[("SC", c1)]))
                st.append(lambda c1=c1, c2=c2: sc.add("dve", lambda e: e.tensor_tensor(out=SCL[:, c2:c2 + 1], in0=SCL[:, c1:c1 + 1], in1=self.LAMT[:, 5:6], op=ALU.mult),
                                                      r=[("SC", c1), "NEGLAM"], w=[("SC", c2)]))
                st.append(lambda Po=Po, c2=c2, qi=qi, pk=pk: sc.add("dve", lambda e: e.scalar_tensor_tensor(
                    out=DT[:, qi, :], in0=Po[:, 0:128], scalar=SCL[:, c2:c2 + 1], in1=N1[:, qi, :], op0=ALU.mult, op1=ALU.add),
                    r=[pk, ("SC", c2), ("N1", qi)], w=[("DT", qi)]))
                st.append(lambda qi=qi, c3=c3: sc.add("act", lambda e: e.activation(out=self.JK2[:, :], in_=DT[:, qi, :], func=AF.Square, accum_out=SCL[:, c3:c3 + 1]),
                                                      r=[("DT", qi)], w=[("SC", c3), "JK2"]))
                st.append(lambda c3=c3, c4=c4: sc.add("act", lambda e: e.activation(out=SCL[:, c4:c4 + 1], in_=SCL[:, c3:c3 + 1], func=AF.Ln, bias=128.0 * EPS),
                                                      r=[("SC", c3)], w=[("SC", c4)]))
                st.append(lambda c4=c4, c5=c5: sc.add("act", lambda e: e.activation(out=SCL[:, c5:c5 + 1], in_=SCL[:, c4:c4 + 1], func=AF.Exp, scale=-0.5),
                                                      r=[("SC", c4)], w=[("SC", c5)]))
                st.append(lambda qi=qi, c5=c5: sc.add("dve", lambda e: e.scalar_tensor_tensor(
                    out=OTL[:, qi, :], in0=DT[:, qi, :], scalar=SCL[:, c5:c5 + 1], in1=self.GSUBK[:, :], op0=ALU.mult, op1=ALU.mult),
                    r=[("DT", qi), ("SC", c5), "GSUBK"], w=[("OTL", qi)]))
            else:
                c1 = self.scol()
                st.append(lambda Po=Po, c1=c1, pk=pk: sc.add("dve", lambda e: e.reciprocal(out=SCL[:, c1:c1 + 1], in_=Po[:, 128:129]), r=[pk], w=[("SC", c1)]))
                st.append(lambda Po=Po, c1=c1, qi=qi, pk=pk: sc.add("dve", lambda e: e.tensor_scalar(
                    out=OTL[:, qi, :], in0=Po[:, 0:128], scalar1=SCL[:, c1:c1 + 1], scalar2=None, op0=ALU.mult),
                    r=[pk, ("SC", c1)], w=[("OTL", qi)]))
        for si in range(max(len(x) for x in steps)):
            for qi in range(4):
                if si < len(steps[qi]):
                    steps[qi][si]()
        if kind == "A" and ui == 0:
            return
        Tb = self.T[0]
        for qi in range(4):
            sc.add("pe", lambda e, qi=qi: e.transpose(out=Tb[:, qi * 128:(qi + 1) * 128], in_=OTL[:, qi, :], identity=self.CMB[:, 0:128]),
                   r=[("OTL", qi), "CMB"], w=[("T", 0)])
        sc.add("dve", lambda e: e.tensor_copy(out=HB[:, hs_out, :], in_=Tb[:, 0:512]), r=[("T", 0)], w=[("HB", hs_out)])

    def merge_phase(self, L, sq):
        sc = self.sc
        BB, BH, FS, HB, PV, WS = self.BB, self.BH, self.FS, self.HB, self.PV, self.WS
        for tg in range(4):
            lops = []
            for c in range(14):
                src = self.oT_sp[sq][c * 128:(c + 1) * 128, tg * 512:(tg + 1) * 512]
                lo = c * 2048 + tg * 512
                lops.append(sc.add("sp", lambda e, lo=lo, src=src: e.dma_start(out=BB[:, lo:lo + 512], in_=src),
                                   r=[("oTsp", sq, c, tg)], w=bkeys(lo, lo + 512), dma=self.st_bb4[tg]))
            batch_final(sc, lops)
        branches = [(0, 4, self.wb_up_a, "up_a"), (4, 4, self.wb_up_b, "up_b"), (8, 6, self.wb_up_c, "up_c")]
        for fp in range(8):
            gslots = []
            for b in range(3):
                gslots.append(self.wfill(L, "in", self.wb_in[L], 0, 16, C_GL + b * 2048 + fp * 256, 256))
            us = self.wi
            self.wi = (us + 1) % NW
            fops = []
            for b, (k0, nk, wbt, nm) in enumerate(branches):
                self.wfill(L, nm, wbt[L], 0, nk, fp * 256, 256, slot=us, k0=k0, part=b)
                fops.append(self.last_fill_op)
            batch_final(sc, fops)
            for mm in range(2):
                f = fp * 2 + mm
                for tg in range(4):
                    acc = None
                    for b, (k0, nk, wbt, nm) in enumerate(branches):
                        zb = (0, 1, 4, 5)[self.mzi % 4]
                        ub = (2, 3, 6)[self.mzi % 3]
                        self.mzi += 1
                        Pz = self.P[zb]
                        gs = gslots[b]
                        for k in range(16):
                            sc.add("pe", lambda e, Pz=Pz, gs=gs, k=k, mm=mm, tg=tg: e.matmul(
                                Pz[:, :], lhsT=WS[:, gs, k, mm * 128:(mm + 1) * 128], rhs=BH[:, k, tg * 512:(tg + 1) * 512],
                                start=(k == 0), stop=(k == 15)),
                                r=self.wkeys(gs) + [("H", k, tg)], w=[("P", zb)])
                        gf = self.fslot()
                        bcol = P_BGATE + b * 16 + f
                        sc.add("act", lambda e, Pz=Pz, gf=gf, bcol=bcol: e.activation(
                            out=FS[:, gf, :], in_=Pz[:, :], func=AF.Sigmoid, bias=PV[:, bcol:bcol + 1]),
                            r=[("P", zb), "PV"], w=[("F", gf)])
                        Pu = self.P[ub]
                        for kk in range(nk):
                            lo = (k0 + kk) * 2048 + tg * 512
                            sc.add("pe", lambda e, Pu=Pu, kk=kk, lo=lo, k0=k0, nk=nk, us=us, mm=mm: e.matmul(
                                Pu[:, :], lhsT=WS[:, us, k0 + kk, mm * 128:(mm + 1) * 128], rhs=BB[:, lo:lo + 512],
                                start=(kk == 0), stop=(kk == nk - 1)),
                                r=[("W", us, b)] + bkeys(lo, lo + 512), w=[("P", ub)])
                        if b == 0:
                            acc = self.fslot()
                            sc.add("dve", lambda e, Pu=Pu, gf=gf, acc=acc: e.tensor_tensor(out=FS[:, acc, :], in0=Pu[:, :], in1=FS[:, gf, :], op=ALU.mult),
                                   r=[("P", ub), ("F", gf)], w=[("F", acc)])
                        else:
                            sc.add("dve", lambda e, Pu=Pu, gf=gf: e.tensor_tensor(out=FS[:, gf, :], in0=Pu[:, :], in1=FS[:, gf, :], op=ALU.mult),
                                   r=[("P", ub), ("F", gf)], w=[("F", gf)])
                            if b == 1:
                                sc.add("dve", lambda e, gf=gf, acc=acc: e.tensor_tensor(out=FS[:, acc, :], in0=FS[:, acc, :], in1=FS[:, gf, :], op=ALU.add),
                                       r=[("F", gf), ("F", acc)], w=[("F", acc)])
                            else:
                                hs = self.hslot()
                                sc.add("dve", lambda e, gf=gf, acc=acc, hs=hs: e.tensor_tensor(out=HB[:, hs, :], in0=FS[:, acc, :], in1=FS[:, gf, :], op=ALU.add),
                                       r=[("F", gf), ("F", acc)], w=[("HB", hs)])
                                dst = self.mT_sp[sq][f * 128:(f + 1) * 128, tg * 512:(tg + 1) * 512]
                                sc.add("act", lambda e, dst=dst, hs=hs: e.dma_start(out=dst, in_=HB[:, hs, :]),
                                       r=[("HB", hs)], w=[("mTsp", sq, f, tg)], dma=self.st_hb[hs])

    def resid_matmul(self, L, sq, name, wbt, r0, xfirst, bg=None, bg_n=0):
        sc = self.sc
        BB, FS, WS = self.BB, self.FS, self.WS
        for mp in range(8):
            slot = self.wfill(L, name, wbt, r0, 16, mp * 256, 256)
            for mm in range(2):
                m = mp * 2 + mm
                for tg in range(4):
                    zb = self.zbank()
                    Pz = self.P[zb]
                    xs = self.fslot()
                    src = self.xsrc(xfirst, sq, m, tg)
                    sc.add("sp", lambda e, xs=xs, src=src: e.dma_start(out=FS[:, xs, :], in_=src),
                           r=[self.xkey(xfirst, sq, m, tg)], w=[("F", xs)], dma=self.st_f[xs])
                    for k in range(16):
                        lo = k * 2048 + tg * 512
                        sc.add("pe", lambda e, Pz=Pz, slot=slot, k=k, lo=lo, mm=mm: e.matmul(
                            Pz[:, :], lhsT=WS[:, slot, k, mm * 128:(mm + 1) * 128], rhs=BB[:, lo:lo + 512],
                            start=(k == 0), stop=(k == 15)),
                            r=self.wkeys(slot) + bkeys(lo, lo + 512), w=[("P", zb)])
                    sc.add("dve", lambda e, Pz=Pz, xs=xs: e.tensor_tensor(out=FS[:, xs, :], in0=Pz[:, :], in1=FS[:, xs, :], op=ALU.add),
                           r=[("P", zb), ("F", xs)], w=[("F", xs)])
                    dst = self.yT[sq][m * 128:(m + 1) * 128, tg * 512:(tg + 1) * 512]
                    sc.add("act", lambda e, dst=dst, xs=xs: e.dma_start(out=dst, in_=FS[:, xs, :]),
                           r=[("F", xs)], w=[("yT", sq, m, tg)], dma=self.st_f[xs])
                    if bg is not None:
                        for _ in range(bg_n):
                            next(bg, None)

    def wout_phase(self, L, sq, first):
        sc = self.sc
        BB = self.BB
        for tg in range(4):
            lops = []
            for c in range(16):
                src = self.mT_sp[sq][c * 128:(c + 1) * 128, tg * 512:(tg + 1) * 512]
                lo = c * 2048 + tg * 512
                lops.append(sc.add("sp", lambda e, lo=lo, src=src: e.dma_start(out=BB[:, lo:lo + 512], in_=src),
                                   r=[("mTsp", sq, c, tg)], w=bkeys(lo, lo + 512), dma=self.st_bb4[tg]))
            batch_final(sc, lops)
        self.resid_matmul(L, sq, "out", self.wb_out[L], 0, first)

    def ffn_phase(self, L, sq, bg=None):
        sc = self.sc
        BB, BH, FS, WS = self.BB, self.BH, self.FS, self.WS
        for qd in range(4):
            for mp in range(8):
                slot = self.wfill(L, "ffi", self.wb_ffi[L], 0, 16, qd * 2048 + mp * 256, 256)
                for mm in range(2):
                    mh = mp * 2 + mm
                    for tg in range(4):
                        zb = self.zbank()
                        Pz = self.P[zb]
                        for k in range(16):
                            sc.add("pe", lambda e, Pz=Pz, slot=slot, k=k, mm=mm, tg=tg: e.matmul(
                                Pz[:, :], lhsT=WS[:, slot, k, mm * 128:(mm + 1) * 128], rhs=BH[:, k, tg * 512:(tg + 1) * 512],
                                start=(k == 0), stop=(k == 15)),
                                r=self.wkeys(slot) + [("H", k, tg)], w=[("P", zb)])
                        rf = self.fslot()
                        sc.add("act", lambda e, Pz=Pz, rf=rf: e.activation(out=FS[:, rf, :], in_=Pz[:, :], func=AF.Relu),
                               r=[("P", zb)], w=[("F", rf)])
                        lo = mh * 2048 + tg * 512
                        sc.add("dve", lambda e, rf=rf, lo=lo: e.tensor_tensor(out=BB[:, lo:lo + 512], in0=FS[:, rf, :], in1=FS[:, rf, :], op=ALU.mult),
                               r=[("F", rf)], w=bkeys(lo, lo + 512))
            if qd == 3 and bg is not None:
                self.resid_matmul(L, sq, "ffo", self.wb_ffo[L], qd * 2048, False, bg=bg, bg_n=2)
                for _ in bg:
                    pass
            else:
                self.resid_matmul(L, sq, "ffo", self.wb_ffo[L], qd * 2048, False)

    def emit_all(self):
        nc = self.nc
        sc = self.sc
        from contextlib import ExitStack
        with ExitStack() as es:
            eng_sems = {k: es.enter_context(nc.semaphore("sem_" + k)) for k in ("pe", "act", "dve", "pool", "sp")}
            stream_sems = [es.enter_context(nc.semaphore("st%d" % i)) for i in range(len(sc.streams))]
            block = es.enter_context(nc.Block())

            @block.tensor
            def _(e):
                sc.emit("pe", e, eng_sems, stream_sems)

            @block.scalar
            def _(e):
                sc.emit("act", e, eng_sems, stream_sems)

            @block.vector
            def _(e):
                sc.emit("dve", e, eng_sems, stream_sems)

            @block.gpsimd
            def _(e):
                sc.emit("pool", e, eng_sems, stream_sems)

            @block.sync
            def _(e):
                sc.emit("sp", e, eng_sems, stream_sems)


def _bf(a):
    return np.ascontiguousarray(a.astype(ml_dtypes.bfloat16))


def host_consts():
    k = np.arange(128)[:, None]
    q = np.arange(128)[None, :]
    masks = [(k <= q).astype(np.float32)]
    for o in range(16):
        d = 128 * o + q - k
        m = ((d >= 0) & (d <= 128)).astype(np.float32)
        m += ((d >= 0) & (d % 4 == 0) & (d <= 512)).astype(np.float32)
        m += ((d >= 0) & (d % 16 == 0) & (d <= 2048)).astype(np.float32)
        masks.append(m)
    cmask = _bf(np.concatenate(masks, axis=1))
    sl_a = 2.0 ** (-8.0 * np.arange(1, 5) / 4)
    sl_c = 2.0 ** (-8.0 * np.arange(1, 7) / 6)
    alibi = np.zeros((128, 160), np.float32)
    kl = np.arange(128, dtype=np.float64)
    for h in range(4):
        for o in range(16):
            alibi[:, h * 16 + o] = (-sl_a[h] * (128 * o + 127 - kl)).astype(np.float32)
    for h in range(6):
        for o in range(16):
            alibi[:, 64 + h * 16 + o] = (-np.float64(np.float32(sl_c[h])) * (128 * o + 127 - kl)).astype(np.float32)
    ident = np.eye(128, dtype=np.float32)
    onesd = np.full((128, 128), 1.0 / 2048, np.float32)
    ones128 = np.full((128, 128), 1.0 / 128, np.float32)
    blk = np.zeros((128, 128), np.float32)
    blk[:64, :64] = 1.0 / 64
    blk[64:, 64:] = 1.0 / 64
    cmatb = _bf(np.concatenate([ident, onesd, ones128, blk], axis=1))
    U = (k <= q).astype(np.float32)
    cmatf = np.ascontiguousarray(np.concatenate([U, np.ones((128, 128), np.float32)], axis=1))
    return cmask, alibi, cmatb, cmatf


def host_pvec(inp):
    pv = np.zeros((DEPTH, 128, NP), np.float32)
    for L in range(DEPTH):
        pv[L, :, P_GMIX:P_GMIX + 16] = inp["g_mix_norm"][L].reshape(16, 128).T
        pv[L, :, P_GFFN:P_GFFN + 16] = inp["g_ffn_norm"][L].reshape(16, 128).T
        pv[L, :, P_BGATE:P_BGATE + 48] = inp["b_gate"][L].reshape(48, 128).T
        pv[L, :, P_GQA] = np.tile(inp["g_q_a"][L], 2)
        pv[L, :, P_GKA] = np.tile(inp["g_k_a"][L], 2)
        pv[L, :, P_GQB] = inp["g_q_b"][L]
        pv[L, :, P_GKB] = inp["g_k_b"][L]
        pv[L, :, P_GQC] = inp["g_q_c"][L]
        pv[L, :, P_GKC] = inp["g_k_c"][L]
        pv[L, :, P_LQ1:P_LQ1 + 64] = inp["lam_q1"][L][None, :]
        pv[L, :, P_LK1:P_LK1 + 64] = inp["lam_k1"][L][None, :]
        pv[L, :, P_LQ2:P_LQ2 + 64] = inp["lam_q2"][L][None, :]
        pv[L, :, P_LK2:P_LK2 + 64] = inp["lam_k2"][L][None, :]
        pv[L, :, P_GSUB:P_GSUB + 128] = inp["g_sub_a"][L][None, :]
        pv[L, :, P_BF:P_BF + 64] = np.tile(inp["b_forget"][L], 16)[None, :]
    return pv


_PROG = {}


def get_program(n_layers=DEPTH, n_seq=2, taps=()):
    key = (n_layers, n_seq, tuple(taps))
    if key not in _PROG:
        _PROG[key] = Builder(n_layers, n_seq, taps).build()
    return _PROG[key]


def make_in_maps(inp, n_cores, n_seq):
    x = np.asarray(inp["x"], np.float32)
    cmask, alibi, cmatb, cmatf = host_consts()
    pv = host_pvec({k: np.asarray(v, np.float32) for k, v in inp.items() if k != "x"})
    shared = {
        "w_in": np.ascontiguousarray(inp["w_in"], dtype=np.float32),
        "w_up_a": np.ascontiguousarray(inp["w_up_a"], dtype=np.float32),
        "w_up_b": np.ascontiguousarray(inp["w_up_b"], dtype=np.float32),
        "w_up_c": np.ascontiguousarray(inp["w_up_c"], dtype=np.float32),
        "w_out": np.ascontiguousarray(inp["w_out"], dtype=np.float32),
        "w_ff_in": np.ascontiguousarray(inp["w_ff_in"], dtype=np.float32),
        "w_ff_out": np.ascontiguousarray(inp["w_ff_out"], dtype=np.float32),
        "pvec": pv, "cmask": cmask, "calibi": alibi, "cmatb": cmatb, "cmatf": cmatf,
    }
    maps = []
    for c in range(n_cores):
        xs = x[c * n_seq:(c + 1) * n_seq]
        m = dict(shared)
        m["xT"] = np.ascontiguousarray(np.transpose(xs, (0, 2, 1)))
        maps.append(m)
    return maps


def kernel(**inputs):
    n_cores, n_seq = 8, 2
    nc = get_program(DEPTH, n_seq)
    maps = make_in_maps(inputs, n_cores, n_seq)
    res = run_bass_kernel_spmd(nc, maps, core_ids=list(range(n_cores)))
    outs = []
    for c in range(n_cores):
        yT = np.asarray(res.results[c]["yT"], dtype=np.float32)
        outs.append(np.transpose(yT, (0, 2, 1)))
    return np.ascontiguousarray(np.concatenate(outs, axis=0))
```
